# Optimizing a Trainium2 kernel written in Bass

```python
import math
import jax, jax.numpy as jnp
from jax import lax
import numpy as np

D_MODEL = 1024
BATCH = 8
SEQ = 4096
DEPTH = 4

N_MIXERS = 2
N_DIFF_LAYERS = (DEPTH + 1) // 2
N_SWA_LAYERS = DEPTH // 2

DIFF_HEADS = 8
DIFF_HEAD_DIM = D_MODEL // DIFF_HEADS // 2
DIFF_Q_BLOCK = 128

SWA_Q_HEADS = 16
SWA_KV_HEADS = 4
SWA_GROUP = SWA_Q_HEADS // SWA_KV_HEADS
SWA_HEAD_DIM = D_MODEL // SWA_Q_HEADS
WINDOW = 128
SWA_BLOCK = WINDOW

ROPE_DIM = 64
ROPE_THETA = 10000.0

D_FF = 2816
CONV_WIDTH = 3
PLE_DIM = 256
EPS = 1e-6

kernel_name = "hybrid_diffattn_swa_convffn_ple_encoder"


def rms_norm(x, g):
    xf = x.astype(jnp.float32)
    y = xf * lax.rsqrt(jnp.mean(xf * xf, axis=-1, keepdims=True) + EPS)
    return (y * g.astype(jnp.float32)).astype(x.dtype)


def rope_tables(positions):
    inv_freq = ROPE_THETA ** (-jnp.arange(0, ROPE_DIM, 2, dtype=jnp.float32) / ROPE_DIM)
    ang = positions.astype(jnp.float32)[..., None] * inv_freq
    return jnp.cos(ang), jnp.sin(ang)


def apply_rope(x, cos, sin):
    xf = x.astype(jnp.float32)
    x1, x2 = jnp.split(xf, 2, axis=-1)
    c, s = cos[:, :, None, :], sin[:, :, None, :]
    return jnp.concatenate([x1 * c - x2 * s, x2 * c + x1 * s], axis=-1).astype(x.dtype)


def diff_attention(h, w_qkv, w_o, lam, subln, cos, sin, layer_idx):
    B, S, _ = h.shape
    H, d = DIFF_HEADS, DIFF_HEAD_DIM
    q, k, v = jnp.split(h @ w_qkv, 3, axis=-1)
    q = apply_rope(q.reshape(B, S, 2 * H, d), cos, sin)
    k = apply_rope(k.reshape(B, S, 2 * H, d), cos, sin)
    v = v.reshape(B, S, H, 2 * d)
    lam_init = 0.8 - 0.6 * math.exp(-0.3 * layer_idx)
    lf = lam.astype(jnp.float32)
    lam_full = jnp.exp(jnp.sum(lf[0] * lf[1])) - jnp.exp(jnp.sum(lf[2] * lf[3])) + lam_init
    scale = d ** -0.5
    nb = S // DIFF_Q_BLOCK
    q_blocks = jnp.moveaxis(q.reshape(B, nb, DIFF_Q_BLOCK, 2 * H, d), 1, 0)

    def one_block(qb):
        s = jnp.einsum('bqhd,bkhd->bhqk', qb, k).astype(jnp.float32) * scale
        pr = jax.nn.softmax(s, axis=-1).reshape(B, H, 2, DIFF_Q_BLOCK, S)
        a = (pr[:, :, 0] - lam_full * pr[:, :, 1]).astype(v.dtype)
        return jnp.einsum('bhqk,bkhe->bqhe', a, v)

    o = lax.map(one_block, q_blocks)
    o = jnp.moveaxis(o, 0, 1).reshape(B, S, H, 2 * d)
    o = rms_norm(o, subln) * (1.0 - lam_init)
    return o.reshape(B, S, H * 2 * d) @ w_o


def windowed_gqa(h, w_qkv, w_o, sinks, cos, sin):
    B, S, _ = h.shape
    Hq, Hkv, G, d, L = SWA_Q_HEADS, SWA_KV_HEADS, SWA_GROUP, SWA_HEAD_DIM, SWA_BLOCK
    nb = S // L
    q, k, v = jnp.split(h @ w_qkv, [Hq * d, Hq * d + Hkv * d], axis=-1)
    q = apply_rope(q.reshape(B, S, Hq, d), cos, sin).reshape(B, nb, L, Hkv, G, d)
    k = apply_rope(k.reshape(B, S, Hkv, d), cos, sin)
    v = v.reshape(B, S, Hkv, d)

    def band(t):
        tp = jnp.pad(t, ((0, 0), (L, L), (0, 0), (0, 0))).reshape(B, nb + 2, L, Hkv, d)
        return jnp.concatenate([tp[:, :-2], tp[:, 1:-1], tp[:, 2:]], axis=2)

    kb, vb = band(k), band(v)
    s = jnp.einsum('bnqhgd,bnchd->bnhgqc', q, kb).astype(jnp.float32) * (d ** -0.5)
    blk = jnp.arange(nb)[:, None, None] * L
    qpos = blk + jnp.arange(L)[None, :, None]
    kpos = blk - L + jnp.arange(3 * L)[None, None, :]
    mask = (jnp.abs(kpos - qpos) <= WINDOW) & (kpos >= 0) & (kpos < S)
    s = jnp.where(mask[None, :, None, None], s, -jnp.inf)
    sink = sinks.astype(jnp.float32).reshape(1, 1, Hkv, G, 1, 1)
    m = jnp.maximum(jnp.max(s, axis=-1, keepdims=True), sink)
    e = jnp.exp(s - m)
    pr = e / (jnp.sum(e, axis=-1, keepdims=True) + jnp.exp(sink - m))
    o = jnp.einsum('bnhgqc,bnchd->bnqhgd', pr.astype(v.dtype), vb).reshape(B, S, Hq * d)
    return o @ w_o


def conv_ffn(h, w_up, conv_w, conv_b, w_down):
    S = h.shape[1]
    u = h @ w_up
    pad = CONV_WIDTH // 2
    up = jnp.pad(u, ((0, 0), (pad, pad), (0, 0)))
    u = sum(up[:, t:t + S] * conv_w[t] for t in range(CONV_WIDTH)) + conv_b
    gate, val = jnp.split(u, 2, axis=-1)
    return (jax.nn.silu(gate) * val) @ w_down


def setup_inputs(seed: int = 0) -> dict:
    key = jax.random.key(seed)
    ks = jax.random.split(key, 24)
    D = D_MODEL

    def nrm(k, shape, scale):
        return jax.random.normal(k, shape, jnp.float32) * scale

    return {
        "x": nrm(ks[0], (BATCH, SEQ, D), 1.0),
        "p": nrm(ks[1], (DEPTH, BATCH, SEQ, PLE_DIM), 1.0),
        "positions": jnp.broadcast_to(jnp.arange(SEQ, dtype=jnp.int32), (BATCH, SEQ)),
        "attn_norm": 1.0 + nrm(ks[2], (DEPTH, D), 0.05),
        "ffn_norm": 1.0 + nrm(ks[3], (DEPTH, D), 0.05),
        "ple_norm": 1.0 + nrm(ks[4], (DEPTH, D), 0.05),
        "final_norm": 1.0 + nrm(ks[5], (D,), 0.05),
        "diff_w_qkv": nrm(ks[6], (N_DIFF_LAYERS, D, 3 * D), D ** -0.5),
        "diff_w_o": nrm(ks[7], (N_DIFF_LAYERS, D, D), D ** -0.5),
        "diff_lambda": nrm(ks[8], (N_DIFF_LAYERS, 4, DIFF_HEAD_DIM), 0.1),
        "diff_subln": 1.0 + nrm(ks[9], (N_DIFF_LAYERS, 2 * DIFF_HEAD_DIM), 0.05),
        "swa_w_qkv": nrm(ks[10], (N_SWA_LAYERS, D, (SWA_Q_HEADS + 2 * SWA_KV_HEADS) * SWA_HEAD_DIM), D ** -0.5),
        "swa_w_o": nrm(ks[11], (N_SWA_LAYERS, SWA_Q_HEADS * SWA_HEAD_DIM, D), D ** -0.5),
        "swa_sinks": nrm(ks[12], (N_SWA_LAYERS, SWA_Q_HEADS), 0.5),
        "ffn_w_up": nrm(ks[13], (DEPTH, D, 2 * D_FF), D ** -0.5),
        "ffn_conv_w": nrm(ks[14], (DEPTH, CONV_WIDTH, 2 * D_FF), CONV_WIDTH ** -0.5),
        "ffn_conv_b": nrm(ks[15], (DEPTH, 2 * D_FF), 0.02),
        "ffn_w_down": nrm(ks[16], (DEPTH, D_FF, D), D_FF ** -0.5),
        "ple_w_proj": nrm(ks[17], (DEPTH, PLE_DIM, D), PLE_DIM ** -0.5),
        "ple_w_gate": nrm(ks[18], (DEPTH, D, D), D ** -0.5),
    }


def reference(x, p, positions, attn_norm, ffn_norm, ple_norm, final_norm,
              diff_w_qkv, diff_w_o, diff_lambda, diff_subln,
              swa_w_qkv, swa_w_o, swa_sinks,
              ffn_w_up, ffn_conv_w, ffn_conv_b, ffn_w_down,
              ple_w_proj, ple_w_gate):
    cos, sin = rope_tables(positions)
    h = x
    for i in range(DEPTH):
        j = i // N_MIXERS
        hn = rms_norm(h, attn_norm[i])
        if i % N_MIXERS == 0:
            mix = diff_attention(hn, diff_w_qkv[j], diff_w_o[j], diff_lambda[j], diff_subln[j], cos, sin, i)
        else:
            mix = windowed_gqa(hn, swa_w_qkv[j], swa_w_o[j], swa_sinks[j], cos, sin)
        h = h + mix
        h = h + conv_ffn(rms_norm(h, ffn_norm[i]), ffn_w_up[i], ffn_conv_w[i], ffn_conv_b[i], ffn_w_down[i])
        gate = jax.nn.sigmoid(rms_norm(h, ple_norm[i]) @ ple_w_gate[i])
        h = h + (p[i] @ ple_w_proj[i]) * gate
    return rms_norm(h, final_norm)
```

```python
import math
from contextlib import ExitStack

import numpy as np
import concourse.bass as bass
import concourse.mybir as mybir
from concourse.bass_utils import run_bass_kernel_spmd

F32 = mybir.dt.float32
BF16 = mybir.dt.bfloat16
I32 = mybir.dt.int32
AF = mybir.ActivationFunctionType
ALU = mybir.AluOpType
AX = mybir.AxisListType

D = 1024
DFF = 2816
NFC = 22
PLE = 256
EPS = 1e-6
TE = 2048
NSP = 480
NSLOT = 5
TWO_PI = 2.0 * math.pi
C1 = 6.28125
C2 = TWO_PI - C1


def _tile_kc(W, cols, nk=None):
    K = W.shape[0]
    kc = K // 128
    t = W[:, cols].reshape(kc, 128, len(cols)).transpose(1, 0, 2).reshape(128, -1)
    out = np.zeros((128, TE), np.float32)
    out[:, : t.shape[1]] = t
    return out


def _rot_cols(cols):
    cols = np.asarray(cols)
    r = cols.reshape(-1, 2, 32)[:, ::-1, :].reshape(-1)
    return r


def swa_chunk_heads(c):
    return (c, 4 + c) if c < 4 else (8 + c - 4, 12 + c - 4)


def pack_layer(lt, j, inp, i):
    out = {}
    ar = np.arange
    if lt == 0:
        wqkv = inp["diff_w_qkv"][j]
        wo = inp["diff_w_o"][j]
        kv = []
        for half in range(2):
            for c in range(4 * half, 4 * half + 4, 2):
                kv.append(_tile_kc(wqkv, 1024 + c * 128 + ar(256)))
            for vt in range(2 * half, 2 * half + 2):
                kv.append(_tile_kc(wqkv, 2048 + vt * 256 + ar(256)))
        a = []
        for half in range(2):
            for c in range(4 * half, 4 * half + 4, 2):
                a.append(_tile_kc(wqkv, c * 128 + ar(256)))
            wo_h = wo[half * 512:(half + 1) * 512]
            for mq in range(2):
                a.append(_tile_kc(wo_h, mq * 512 + ar(512)))
    else:
        wqkv = inp["swa_w_qkv"][j]
        wo = inp["swa_w_o"][j]
        kv = []
        kv.append(_tile_kc(wqkv, 1024 + ar(256)))
        kv.append(_tile_kc(wqkv, 1280 + ar(256)))
        a = []
        rowperm = []
        qcols = []
        for c in range(8):
            hA, hB = swa_chunk_heads(c)
            cols = np.concatenate([hA * 64 + ar(64), hB * 64 + ar(64)])
            rowperm.append(cols)
            qcols.append(cols)
        for c in range(0, 8, 2):
            a.append(_tile_kc(wqkv, np.concatenate([qcols[c], qcols[c + 1]])))
        wo_p = wo[np.concatenate(rowperm)]
        for mp in range(4):
            a.append(_tile_kc(wo_p, mp * 256 + ar(256)))
    out["wkv"] = np.stack(kv)
    out["wa"] = np.stack(a)
    b = []
    wup = inp["ffn_w_up"][i]
    for jf in range(NFC):
        b.append(_tile_kc(wup, np.concatenate([jf * 128 + ar(128), DFF + jf * 128 + ar(128)])))
    wdn = inp["ffn_w_down"][i]
    for m in range(8):
        for half in range(2):
            b.append(_tile_kc(wdn[half * 11 * 128:(half + 1) * 11 * 128], m * 128 + ar(128)))
    wg = inp["ple_w_gate"][i]
    for mp in range(4):
        b.append(_tile_kc(wg, mp * 256 + ar(256)))
    b.append(_tile_kc(inp["ple_w_proj"][i], ar(1024)))
    out["wb"] = np.stack(b)
    sp = np.zeros((128, NSP), np.float32)
    sp[:, 0:8] = inp["attn_norm"][i].reshape(8, 128).T
    sp[:, 8:16] = inp["ffn_norm"][i].reshape(8, 128).T
    sp[:, 16:24] = inp["ple_norm"][i].reshape(8, 128).T
    sp[:, 24:32] = inp["final_norm"].reshape(8, 128).T
    cw = inp["ffn_conv_w"][i]
    for t in range(3):
        sp[:, 32 + 44 * t: 32 + 44 * (t + 1)] = cw[t].reshape(44, 128).T
    sp[:, 164:208] = inp["ffn_conv_b"][i].reshape(44, 128).T
    if lt == 0:
        sp[:, 208] = inp["diff_subln"][j]
        sp[:, 217:473] = inp["diff_lambda"][j].reshape(1, 256)
    else:
        sk = inp["swa_sinks"][j]
        for c in range(8):
            hA, hB = swa_chunk_heads(c)
            sp[0:64, 209 + c] = sk[hA]
            sp[64:128, 209 + c] = sk[hB]
    out["sp"] = sp
    return out


def host_consts():
    cst = np.zeros((128, 8), np.float32)
    inv_freq = (10000.0 ** (-np.arange(0, 64, 2, dtype=np.float32) / np.float32(64))).astype(np.float32)
    p = np.arange(128)
    cst[:, 0] = inv_freq[p % 32]
    cst[:, 1] = np.where((p % 64) < 32, -1.0, 1.0)
    cst[:, 2] = EPS
    kk = np.arange(128)[:, None]
    qq = np.arange(512)[None, :]
    masks = np.zeros((128, 6, 512), np.float32)
    for r in range(6):
        rel = r - 1
        masks[:, r, :] = np.where(np.abs(qq - kk - 128 * rel) <= 128, 0.0, -30000.0).astype(np.float32)
    ident = np.eye(128, dtype=np.float32)
    return cst, np.concatenate([masks.reshape(128, 6 * 512), ident], axis=1)


def host_perm():
    p = np.arange(128)
    partner = np.where((p % 64) < 32, p + 32, p - 32)
    pm = np.zeros((128, 128), np.float32)
    pm[partner, p] = 1.0
    return pm


class Eng:
    def __init__(self, nc, es, eng, name, own_sem=True):
        self.e = eng
        self.name = name
        self.sem = es.enter_context(nc.semaphore("s_" + name)) if own_sem else None
        self.cnt = 0
        self.waited = {}
        self.last = None

    def wait(self, tok):
        if tok is None:
            return
        key, sem, val = tok
        if self.waited.get(key, 0) >= val:
            return
        self.e.wait_ge(sem, val)
        self.waited[key] = val

    def waits(self, toks):
        for t in toks:
            self.wait(t)

    def done(self, ins):
        ins.then_inc(self.sem, 1)
        self.cnt += 1
        self.last = (self.name, self.sem, self.cnt)
        return self.last


class DSem:
    def __init__(self, nc, es, name, registry):
        self.sem = es.enter_context(nc.semaphore("d_" + name))
        self.name = "d_" + name
        self.cnt = 0
        registry.append(self)

    def tok(self):
        return (self.name, self.sem, self.cnt)

    def fire(self, ins):
        ins.then_inc(self.sem, 16)
        self.cnt += 16
        return self.tok()


class Fifo:
    def __init__(self, items):
        self.free = [(it, []) for it in items]

    def alloc(self):
        assert self.free, "pool exhausted"
        return self.free.pop(0)

    def release(self, item, toks):
        self.free.append((item, list(toks)))


def window_list(S):
    n = -(-S // 510)
    base = S // n
    rem = S - base * n
    ws = []
    s = 0
    for i in range(n):
        w = base + (1 if i < rem else 0)
        ws.append((s, s + w))
        s += w
    assert s == S
    return ws


def build_program(S, layers, apply_final, n_win=2):
    nc = bass.Bass("TRN2", target_bir_lowering=False)
    L = len(layers)
    NT = S // 128
    NB = S // 512
    dr = {}
    dr["xT"] = nc.dram_tensor("xT", [D, S], F32, kind="ExternalInput").ap()
    dr["pT"] = nc.dram_tensor("pT", [L, PLE, S], F32, kind="ExternalInput").ap()
    dr["pos"] = nc.dram_tensor("pos", [1, S], I32, kind="ExternalInput").ap()
    dr["cst"] = nc.dram_tensor("cst", [128, 8], F32, kind="ExternalInput").ap()
    dr["msk"] = nc.dram_tensor("msk", [128, 6 * 512 + 128], F32, kind="ExternalInput").ap()
    dr["perm"] = nc.dram_tensor("perm", [128, 128], F32, kind="ExternalInput").ap()
    dr["outT"] = nc.dram_tensor("outT", [D, S], F32, kind="ExternalOutput").ap()
    hX = nc.dram_tensor("hX", [D, S], F32, kind="Internal").ap()
    hY = nc.dram_tensor("hY", [D, S], F32, kind="Internal").ap()
    tabC = nc.dram_tensor("tabC", [128, S], F32, kind="Internal").ap()
    tabS = nc.dram_tensor("tabS", [128, S], F32, kind="Internal").ap()
    wsrc, wbf = [], []
    for li, (gi, lt) in enumerate(layers):
        nkv = 8 if lt == 0 else 2
        na = 8 if lt == 0 else 8
        d = {}
        e = {}
        for nm, n in (("wkv", nkv), ("wa", na), ("wb", 43)):
            d[nm] = nc.dram_tensor(f"{nm}{li}", [n, 128, TE], F32, kind="ExternalInput").ap()
            e[nm] = nc.dram_tensor(f"{nm}b{li}", [n, 128, TE], BF16, kind="Internal").ap()
        d["sp"] = nc.dram_tensor(f"sp{li}", [128, NSP], F32, kind="ExternalInput").ap()
        wsrc.append(d)
        wbf.append(e)

    es = ExitStack()
    with es:
        dsems = []
        PE = Eng(nc, es, nc.tensor, "pe")
        ACT = Eng(nc, es, nc.scalar, "act")
        DVE = Eng(nc, es, nc.vector, "dve")
        POOL = Eng(nc, es, nc.gpsimd, "pool")
        SP = Eng(nc, es, nc.sync, "sp", own_sem=False)
        ENGS = [PE, ACT, DVE, POOL]

        uid = [0]

        def sb(name, shape, dt, stack=es):
            uid[0] += 1
            return stack.enter_context(nc.sbuf_tensor(f"{name}_{uid[0]}", shape, dt))

        def op(E, deps, fn):
            E.waits(deps)
            return E.done(fn(E.e))

        def dma(Q, deps, ds, out, in_):
            Q.waits(deps)
            return ds.fire(Q.e.dma_start(out=out, in_=in_))

        def barrier():
            toks = [E.last for E in ENGS if E.last is not None] + [d.tok() for d in dsems if d.cnt > 0 and not d.name.startswith('d_cast')]
            for E in ENGS + [SP]:
                E.waits(toks)

        cst = sb("cst", [128, 8], F32)
        spt = [sb(f"spt{li}", [128, NSP], F32) for li in range(L)]
        ones = sb("ones", [128, 128], BF16)
        onesD = sb("onesD", [128, 128], BF16)
        onesH = sb("onesH", [128, 128], BF16)
        onesF = sb("onesF", [128, 128], F32)
        permT = sb("permT", [128, 128], BF16)
        wring = [sb(f"wr{i}", [128, TE], BF16) for i in range(NSLOT)]
        sq_items = [sb(f"sq{i}", [128, 512], BF16) for i in range(3)]
        rs = sb("rs", [128, 512], F32)
        es_t = sb("es_t", [128, 8], F32)
        lam_t = sb("lam_t", [128, 8], F32)
        lamw = sb("lamw", [128, 128], F32)
        psum = [es.enter_context(nc.psum_tensor(f"ps{i}", [128, 1024], F32)) for i in range(4)]
        pp = Fifo(psum)
        sqr = Fifo(sq_items)

        d_misc = DSem(nc, es, "misc", dsems)
        d_cast = [[DSem(nc, es, f"cast{li}_{k}", dsems) for k in range(3)] for li in range(L)]
        d_w = [DSem(nc, es, f"w{i}", dsems) for i in range(NSLOT)]
        d_h = [DSem(nc, es, f"h{i}", dsems) for i in range(2)]
        d_tab = DSem(nc, es, "tab", dsems)
        d_st = [DSem(nc, es, f"st{i}", dsems) for i in range(2)]
        d_p = [DSem(nc, es, f"p{i}", dsems) for i in range(2)]

        t_c = dma(SP, [], d_misc, cst[:], dr["cst"])
        perm_ready = dma(POOL, [], d_misc, permT[:], dr["perm"])
        for li in range(L):
            t_c = dma(SP, [], d_misc, spt[li][:], wsrc[li]["sp"])
        cast_tok = []
        cast_plan = []
        for li in range(L):
            toks = []
            plans = []
            for k, nm in enumerate(("wkv", "wa", "wb")):
                n = wsrc[li][nm].shape[0]
                src_ = wsrc[li][nm].rearrange("n p f -> (n p) f")
                dst_ = wbf[li][nm].rearrange("n p f -> (n p) f")
                i0 = 0
                pl = []
                while i0 < n:
                    i1 = min(n, i0 + 8)
                    pl.append((dst_[i0 * 128:i1 * 128, :], src_[i0 * 128:i1 * 128, :]))
                    i0 = i1
                plans.append(pl)
                toks.append((d_cast[li][k].name, d_cast[li][k].sem, 16 * len(pl)))
            cast_tok.append(toks)
            cast_plan.append(plans)

        def issue_casts(li_):
            for k in range(3):
                for (dd, ss) in cast_plan[li_][k]:
                    dma(POOL, [], d_cast[li_][k], dd, ss)

        cast_queue = []

        def queue_casts(li_):
            for k in range(3):
                for (dd, ss) in cast_plan[li_][k]:
                    cast_queue.append((d_cast[li_][k], dd, ss))

        def pop_cast():
            if cast_queue:
                ds_, dd, ss = cast_queue.pop(0)
                dma(POOL, [], ds_, dd, ss)

        issue_casts(0)
        c_ready = t_c
        t0 = op(DVE, [], lambda e: e.memset(ones[:], 1.0))
        t0 = op(DVE, [], lambda e: e.memset(onesD[:], 1.0 / 1024))
        t0 = op(DVE, [], lambda e: e.memset(onesF[:], 1.0))
        const_ready = op(DVE, [], lambda e: e.memset(onesH[:], 1.0 / 128))

        invf = cst[:, 0:1]
        sgn = cst[:, 1:2]
        eps_ap = cst[:, 2:3]
        with ExitStack() as ts:
            posi = sb("posi", [128, S], I32, ts)
            tA = sb("tA", [128, S], F32, ts)
            tB = sb("tB", [128, S], F32, ts)
            tK = sb("tK", [128, S], F32, ts)
            tR = sb("tR", [128, S], F32, ts)
            tO = [sb(f"tO{i}", [128, S], F32, ts) for i in range(2)]
            ki = sb("ki", [128, S], I32, ts)
            last_tab = None
            tl = dma(SP, [], d_tab, posi[:], dr["pos"][0:1, :].partition_broadcast(128))
            t = op(DVE, [tl, c_ready], lambda e: e.tensor_copy(out=tA[:], in_=posi[:]))
            t = op(DVE, [t], lambda e: e.tensor_scalar(out=tA[:], in0=tA[:], scalar1=invf, scalar2=None, op0=ALU.mult))
            for which in range(2):
                if which == 1:
                    t = op(DVE, [t], lambda e: e.tensor_scalar(out=tA[:], in0=tA[:], scalar1=float(math.pi / 2), scalar2=None, op0=ALU.add))
                t = op(DVE, [t], lambda e: e.tensor_scalar(out=tB[:], in0=tA[:], scalar1=float(1.0 / TWO_PI), scalar2=None, op0=ALU.mult))
                t = op(DVE, [t], lambda e: e.tensor_copy(out=ki[:], in_=tB[:]))
                t = op(DVE, [t], lambda e: e.tensor_copy(out=tK[:], in_=ki[:]))
                t = op(DVE, [t, ACT.last], lambda e: e.scalar_tensor_tensor(out=tR[:], in0=tK[:], scalar=-C1, in1=tA[:], op0=ALU.mult, op1=ALU.add))
                t = op(DVE, [t], lambda e: e.scalar_tensor_tensor(out=tR[:], in0=tK[:], scalar=-C2, in1=tR[:], op0=ALU.mult, op1=ALU.add))
                t = op(DVE, [t], lambda e: e.tensor_scalar(out=tB[:], in0=tR[:], scalar1=float(math.pi), scalar2=float(-TWO_PI), op0=ALU.is_gt, op1=ALU.mult))
                t = op(DVE, [t], lambda e: e.tensor_tensor(out=tR[:], in0=tR[:], in1=tB[:], op=ALU.add))
                t = op(DVE, [t], lambda e: e.tensor_scalar(out=tR[:], in0=tR[:], scalar1=float(math.pi), scalar2=float(-math.pi), op0=ALU.min, op1=ALU.max))
                to = tO[which]
                ta = op(ACT, [t], lambda e: e.activation(out=to[:], in_=tR[:], func=AF.Sin))
                if which == 0:
                    ta = op(DVE, [ta], lambda e: e.tensor_scalar(out=to[:], in0=to[:], scalar1=sgn, scalar2=None, op0=ALU.mult))
                t = ta
                last_tab = dma(SP, [ta], d_tab, (tabS if which == 0 else tabC)[:, :], to[:])
            barrier()
        tab_ready = last_tab

        class Feeder:
            def __init__(self):
                self.seq = []
                self.issued = 0
                self.pos = 0
                self.rel = {}
                self.ready = {}

            def plan(self, tiles):
                self.seq.extend(tiles)

            def _issue(self, i):
                ap, ctok = self.seq[i]
                deps = [ctok]
                if i >= NSLOT:
                    deps += self.rel[i - NSLOT]
                self.ready[i] = dma(SP, deps, d_w[i % NSLOT], wring[i % NSLOT][:], ap)
                self.issued = i + 1

            def start(self):
                for i in range(min(NSLOT, len(self.seq))):
                    self._issue(i)

            def next(self):
                i = self.pos
                self.pos += 1
                return i, wring[i % NSLOT], self.ready[i]

            def release(self, i, toks):
                self.rel[i] = list(toks)
                if i + NSLOT < len(self.seq):
                    self._issue(i + NSLOT)

        feeder = Feeder()

        def hview(ap):
            return ap.rearrange("(c p) t -> p c t", p=128)

        def norm_block(h, W, g0, spl, hn, deps_h, deps_hn, off=0, out_f32=None):
            ps, pfree = pp.alloc()
            PE.waits(pfree)
            tmm = None
            for c in range(8):
                sq, sfree = sqr.alloc()
                tsq = op(POOL, deps_h + sfree, lambda e: e.tensor_tensor(out=sq[:, 0:W], in0=h[:, c, off:off + W], in1=h[:, c, off:off + W], op=ALU.mult))
                PE.wait(tsq)
                PE.wait(const_ready)
                tmm = PE.done(PE.e.matmul(ps[:, 0:W], onesD[:], sq[:, 0:W], start=(c == 0), stop=(c == 7)))
                sqr.release(sq, [tmm])
            t1 = op(ACT, [tmm, DVE.last], lambda e: e.activation(out=rs[:, 0:W], in_=ps[:, 0:W], func=AF.Ln, bias=eps_ap, scale=1.0))
            pp.release(ps, [t1])
            t2 = op(ACT, [t1], lambda e: e.activation(out=rs[:, 0:W], in_=rs[:, 0:W], func=AF.Exp, scale=-0.5))
            t = None
            for c in range(8):
                if out_f32 is None:
                    t = op(DVE, [t2] + deps_h + deps_hn, lambda e: e.scalar_tensor_tensor(out=hn[:, c, 0:W], in0=h[:, c, off:off + W], scalar=spl[:, g0 + c:g0 + c + 1], in1=rs[:, 0:W], op0=ALU.mult, op1=ALU.mult))
                else:
                    t = op(DVE, [t2] + deps_h + deps_hn, lambda e: e.scalar_tensor_tensor(out=out_f32[:, c, off:off + W], in0=h[:, c, off:off + W], scalar=spl[:, g0 + c:g0 + c + 1], in1=rs[:, 0:W], op0=ALU.mult, op1=ALU.mult))
            return t

        def norm_squares(h, W, sq8, deps, use_act, off=0):
            toks = []
            for c in range(8):
                if use_act and c % 2 == 1:
                    toks.append(op(ACT, deps, lambda e: e.activation(out=sq8[:, c, 0:W], in_=h[:, c, off:off + W], func=AF.Square)))
                else:
                    toks.append(op(POOL, deps, lambda e: e.tensor_tensor(out=sq8[:, c, 0:W], in0=h[:, c, off:off + W], in1=h[:, c, off:off + W], op=ALU.mult)))
            return toks

        def norm_finish(h, W, g0, spl, hn, sq8, sq_toks, deps_h, deps_hn, off=0):
            ps, pfree = pp.alloc()
            PE.waits(pfree + sq_toks + [const_ready])
            ins = None
            for c in range(8):
                ins = PE.e.matmul(ps[:, 0:W], onesD[:], sq8[:, c, 0:W], start=(c == 0), stop=(c == 7))
            tmm = PE.done(ins)
            t1 = op(ACT, [tmm, DVE.last], lambda e: e.activation(out=rs[:, 0:W], in_=ps[:, 0:W], func=AF.Ln, bias=eps_ap, scale=1.0))
            pp.release(ps, [t1])
            t2 = op(ACT, [t1], lambda e: e.activation(out=rs[:, 0:W], in_=rs[:, 0:W], func=AF.Exp, scale=-0.5))
            t = None
            for c in range(8):
                t = op(DVE, [t2] + deps_h + deps_hn, lambda e: e.scalar_tensor_tensor(out=hn[:, c, 0:W], in0=h[:, c, off:off + W], scalar=spl[:, g0 + c:g0 + c + 1], in1=rs[:, 0:W], op0=ALU.mult, op1=ALU.mult))
            return t, tmm

        steps = []

        def tiles_of(li, nm, idxs):
            k = {"wkv": 0, "wa": 1, "wb": 2}[nm]
            return [(wbf[li][nm][i], cast_tok[li][k]) for i in idxs]

        for li, (gi, lt) in enumerate(layers):
            if lt == 0:
                for half in range(2):
                    for b in range(NB):
                        feeder.plan(tiles_of(li, "wkv", range(4 * half, 4 * half + 4)))
                    qt_ = list(range(4 * half, 4 * half + 2))
                    wo_ = list(range(4 * half + 2, 4 * half + 4))
                    feeder.plan(tiles_of(li, "wa", qt_))
                    for b in range(NB):
                        if b + 1 < NB:
                            feeder.plan(tiles_of(li, "wa", qt_))
                        feeder.plan(tiles_of(li, "wa", wo_))
            else:
                for b in range(NB):
                    feeder.plan(tiles_of(li, "wkv", range(2)))
                feeder.plan(tiles_of(li, "wa", range(4)))
                for b in range(NB):
                    if b + 1 < NB:
                        feeder.plan(tiles_of(li, "wa", range(4)))
                    feeder.plan(tiles_of(li, "wa", range(4, 8)))
            wl = window_list(S)
            for p0 in range(0, len(wl), n_win):
                feeder.plan(tiles_of(li, "wb", range(43)))
        feeder.start()

        for li, (gi, lt) in enumerate(layers):
            spl = spt[li]
            src = dr["xT"] if li == 0 else hX
            is_last = (li == L - 1)
            lam_init = 0.8 - 0.6 * math.exp(-0.3 * gi)
            halves = [0, 1] if lt == 0 else [0]
            NQC = 4 if lt == 0 else 8
            NKC = 4 if lt == 0 else 2
            VW = 512 if lt == 0 else 256
            NVT = VW // 256

            if lt == 0:
                l = spl[:, 217:473]
                t = op(DVE, [c_ready], lambda e: e.tensor_tensor(out=lamw[:, 0:64], in0=l[:, 0:64], in1=l[:, 64:128], op=ALU.mult))
                t = op(DVE, [c_ready], lambda e: e.tensor_tensor(out=lamw[:, 64:128], in0=l[:, 128:192], in1=l[:, 192:256], op=ALU.mult))
                t = op(DVE, [t], lambda e: e.tensor_reduce(out=lam_t[:, 0:1], in_=lamw[:, 0:64], axis=AX.X, op=ALU.add))
                t = op(DVE, [t], lambda e: e.tensor_reduce(out=lam_t[:, 1:2], in_=lamw[:, 64:128], axis=AX.X, op=ALU.add))
                t = op(ACT, [t], lambda e: e.activation(out=lam_t[:, 2:4], in_=lam_t[:, 0:2], func=AF.Exp))
                t = op(DVE, [t], lambda e: e.scalar_tensor_tensor(out=lam_t[:, 4:5], in0=lam_t[:, 3:4], scalar=float(-lam_init), in1=lam_t[:, 2:3], op0=ALU.add, op1=ALU.subtract))
                t = op(DVE, [t], lambda e: e.tensor_scalar(out=lam_t[:, 5:6], in0=spl[:, 208:209], scalar1=float(1.0 - lam_init), scalar2=None, op0=ALU.mult))
                lay_ready = t
                nlam = lam_t[:, 4:5]
                sublnS = lam_t[:, 5:6]
            else:
                lay_ready = op(ACT, [c_ready], lambda e: e.activation(out=es_t[:, 0:8], in_=spl[:, 209:217], func=AF.Exp))

            with ExitStack() as asc:
                KT = sb("KT", [128, NKC, S], BF16, asc)
                V = sb("V", [128, NT, VW], BF16, asc)
                hbs = [sb("hb", [128, 8, 512], F32, asc) for _ in range(2)]
                hn = sb("hn", [128, 8, 512], BF16, asc)
                sq8 = sb("sq8", [128, 8, 512], BF16, asc)
                QT = sb("QT", [128, NQC, 512], BF16, asc)
                OTall = sb("OT", [128, 2 * NQC, 512], BF16, asc)
                OTs = [OTall[:, 0:NQC, :], OTall[:, NQC:2 * NQC, :]]
                hnb = [hn, OTall[:, 0:8, :]]
                ctab = sb("ctab", [128, 512], F32, asc)
                stab = sb("stab", [128, 512], F32, asc)
                rt_items = [(sb(f"rt{i}a", [128, 512], F32, asc), sb(f"rt{i}b", [128, 512], F32, asc)) for i in range(2)]
                rtr = Fifo(rt_items)
                qbr = Fifo([sb(f"qb{i}", [128, 512], BF16, asc) for i in range(2)])
                p_items = [sb(f"P{i}", [128, 1024], BF16, asc) for i in range(3 if lt == 0 else 4)]
                pr = Fifo(p_items)
                rec = sb("rec", [128, 1024], F32, asc)
                o12 = sb("o12", [128, 1024], F32, asc)
                if lt == 0:
                    ods = [sb("od", [128, NQC, 512], F32, asc) for _ in range(2)]
                    accs = [sb("acc", [128, 512], F32, asc) for _ in range(2)]
                    acc_state = {"n": 0, "free": [[], []]}
                else:
                    maskf = sb("mask", [128, 6 * 512 + 128], BF16, asc)
                    tmk = dma(POOL, [], d_misc, maskf[:], dr["msk"])
                    mask = maskf[:, 0:3072].rearrange("p (r q) -> p r q", r=6)
                    ident = maskf[:, 3072:3200]

                A = {"hb_free": [[], []], "hn_free": [], "qt_free": [], "ot_free": [[], []], "od_free": [[], []],
                     "tab_free": [], "sq_free": [], "kt_free": []}

                def load_h(b, srcap):
                    cs = slice(b * 512, (b + 1) * 512)
                    return dma(SP, A["hb_free"][b % 2], d_h[b % 2], hbs[b % 2][:], hview(srcap)[:, :, cs])

                def load_tabs(b):
                    cs = slice(b * 512, (b + 1) * 512)
                    dma(SP, A["tab_free"] + [tab_ready], d_tab, ctab[:], tabC[:, cs])
                    return dma(SP, [], d_tab, stab[:], tabS[:, cs])

                def proj_rope(wt, wtok, hn_tok, tt, dst_ap, extra_dst_deps, hn=hn, wcol=0):
                    w3 = wt[:, 0:2048].rearrange("p (c n) -> p c n", c=8)
                    ps, pfree = pp.alloc()
                    PE.waits(pfree + [wtok, hn_tok])
                    for c in range(8):
                        ins = PE.e.matmul(ps[:, 0:512], w3[:, c, wcol * 128:(wcol + 1) * 128], hn[:, c, :], start=(c == 0), stop=(c == 7))
                    tm0 = PE.done(ins)
                    qb, qfree = qbr.alloc()
                    tq = op(ACT, [tm0] + qfree, lambda e: e.activation(out=qb[:], in_=ps[:, 0:512], func=AF.Copy))
                    PE.waits([tq, perm_ready])
                    tm = PE.done(PE.e.matmul(ps[:, 512:1024], permT[:], qb[:], start=True, stop=True))
                    qbr.release(qb, [tm])
                    (ra, rb), rfree = rtr.alloc()
                    t1 = op(DVE, [tm0, tq, tt] + rfree, lambda e: e.tensor_tensor(out=ra[:], in0=ps[:, 0:512], in1=ctab[:], op=ALU.mult))
                    t2 = op(DVE, [tm, tt] + rfree, lambda e: e.tensor_tensor(out=rb[:], in0=ps[:, 512:1024], in1=stab[:], op=ALU.mult))
                    pp.release(ps, [t1, t2])
                    t3 = op(POOL, [t1, t2] + extra_dst_deps, lambda e: e.tensor_tensor(out=dst_ap, in0=ra[:], in1=rb[:], op=ALU.add))
                    rtr.release((ra, rb), [t3])
                    return tm0, t3

                for half in halves:
                    kt_tok = None
                    v_tok = None
                    hnfree = [list(A["hn_free"]), list(A["ot_free"][0]) + list(A["ot_free"][1])]
                    thd = {0: load_h(0, src)}
                    tt = load_tabs(0)
                    sqt = norm_squares(hbs[0], 512, sq8, [thd[0]] + A["sq_free"], True)
                    tn, tmm = norm_finish(hbs[0], 512, 0, spl, hnb[0], sq8, sqt, [thd[0]], hnfree[0])
                    A["sq_free"] = [tmm]
                    A["hb_free"][0] = [tn]
                    if NB > 1:
                        thd[1] = load_h(1, src)
                    lastmm = None
                    for b in range(NB):
                        cs = slice(b * 512, (b + 1) * 512)
                        hcur = hnb[b % 2]
                        if b + 1 < NB:
                            sqt_n = norm_squares(hbs[(b + 1) % 2], 512, sq8, [thd[b + 1]] + A["sq_free"], True)
                        for kp in range(NKC // 2):
                            wi, wt, wtok = feeder.next()
                            for j_ in range(2):
                                tm, t3 = proj_rope(wt, wtok, tn, tt, KT[:, 2 * kp + j_, cs], A["kt_free"], hn=hcur, wcol=j_)
                                kt_tok = t3
                                lastmm = tm
                            feeder.release(wi, [tm])
                        A["tab_free"] = [DVE.last]
                        if b + 1 < NB:
                            tt_n = load_tabs(b + 1)
                            tn_n, tmm = norm_finish(hbs[(b + 1) % 2], 512, 0, spl, hnb[(b + 1) % 2], sq8, sqt_n, [thd[b + 1]], hnfree[(b + 1) % 2])
                            A["sq_free"] = [tmm]
                            A["hb_free"][(b + 1) % 2] = [tn_n]
                            if b + 2 < NB:
                                thd[b + 2] = load_h(b + 2, src)
                        for vt in range(NVT):
                            wi, wt, wtok = feeder.next()
                            w3 = wt[:, 0:2048].rearrange("p (c n) -> p c n", c=8)
                            ps, pfree = pp.alloc()
                            PE.waits(pfree + [wtok, tn])
                            ins = None
                            for tq in range(4):
                                for c in range(8):
                                    ins = PE.e.matmul(ps[:, tq * 256:(tq + 1) * 256], hcur[:, c, tq * 128:(tq + 1) * 128], w3[:, c, :], start=(c == 0), stop=(c == 7))
                            tm = PE.done(ins)
                            feeder.release(wi, [tm])
                            te = op(ACT, [tm] + A["kt_free"], lambda e: e.activation(out=V[:, b * 4:(b + 1) * 4, vt * 256:(vt + 1) * 256], in_=ps[:, 0:1024].rearrange("p (a n) -> p a n", a=4), func=AF.Copy))
                            pp.release(ps, [te])
                            v_tok = te
                            lastmm = tm
                        hnfree[b % 2] = [lastmm]
                        if b + 1 < NB:
                            tn, tt = tn_n, tt_n
                    A["hn_free"] = [lastmm]
                    A["ot_free"] = [[lastmm], [lastmm]]
                    kv_ready = [kt_tok, v_tok, POOL.last, ACT.last]

                    use_acc = (lt == 0 and half == 1)
                    groups = [(q_,) for q_ in range(NQC)] if lt == 0 else [(2 * q_, 2 * q_ + 1) for q_ in range(NQC // 2)]
                    NG = len(groups)

                    def pro_norm(b, th, sqt):
                        tn, tmm = norm_finish(hbs[b % 2], 512, 0, spl, hn, sq8, sqt, [th], A["hn_free"])
                        A["sq_free"] = [tmm]
                        return tn

                    def pro_q(b, tn, th, tt):
                        cs = slice(b * 512, (b + 1) * 512)
                        qtoks = []
                        lastq = None
                        for qp in range(NQC // 2):
                            wi, wt, wtok = feeder.next()
                            for j_ in range(2):
                                tm, t3 = proj_rope(wt, wtok, tn, tt, QT[:, 2 * qp + j_, :], A["qt_free"], wcol=j_)
                                qtoks.append(t3)
                                lastq = tm
                            feeder.release(wi, [tm])
                        A["tab_free"] = [DVE.last]
                        A["hn_free"] = [lastq]
                        if use_acc:
                            th = dma(SP, [tn], d_h[b % 2], hbs[b % 2][:], hview(hY)[:, :, cs])
                        return th, qtoks

                    def attention_group(b, qcs, qtoks):
                        OT = OTs[b % 2]
                        items = []
                        for qc in qcs:
                            if lt == 0:
                                ks = [(kc, 0, 512, 0) for kc in range(NT)]
                            else:
                                base = b * 4
                                rng = {-1: (0, 128), 0: (0, 256), 1: (0, 512), 3: (256, 512), 4: (384, 512), 2: (0, 512)}
                                ks = [(base + r, rng[r][0], rng[r][1], r + 1) for r in (1, -1, 0, 3, 4, 2) if 0 <= base + r < NT]
                            for i_, k_ in enumerate(ks):
                                items.append((qc, k_, i_ == 0, i_ == len(ks) - 1))
                        accd = {}
                        last_sc = [None]

                        def emit_av(pd):
                            qc, (kc, qa, qb, r), first, last, P, tokp, tacc = pd
                            ac = accd[qc]
                            O_db = ac["O"]
                            PE.waits(tokp + (ac["free"] if first else []) + kv_ready)
                            PE.wait(const_ready)
                            if lt == 0:
                                R_db = ac["R"]
                                vl = V[:, kc, qc * 128:(qc + 1) * 128]
                                PE.e.matmul(O_db[:, 0:512], vl, P[:, 0:512], start=first, stop=last)
                                PE.e.matmul(O_db[:, 512:1024], vl, P[:, 512:1024], start=first, stop=last)
                                ins = PE.e.matmul(R_db[:, 512:1024], ones[:], P[:, 512:1024], start=first, stop=last)
                            else:
                                gA = 0 if qc < 4 else 2
                                gB = gA + 1
                                PE.e.matmul(O_db[0:64, qa:qb], V[:, kc, gA * 64:(gA + 1) * 64], P[:, qa:qb], start=first, stop=last)
                                PE.e.matmul(O_db[64:128, qa:qb], V[:, kc, gB * 64:(gB + 1) * 64], P[:, 512 + qa:512 + qb], start=first, stop=last)
                                PE.e.matmul(O_db[0:64, 512 + qa:512 + qb], ones[:, 0:64], P[:, qa:qb], start=first, stop=last)
                                ins = PE.e.matmul(O_db[64:128, 512 + qa:512 + qb], ones[:, 0:64], P[:, 512 + qa:512 + qb], start=first, stop=last)
                            tk = PE.done(ins)
                            pr.release(P, [tk] + ([tacc] if tacc is not None else []))
                            if last:
                                epilogue(qc, tk, tacc)

                        def epilogue(qc, av_tok, tacc):
                            ac = accd[qc]
                            O_db = ac["O"]
                            if lt == 0:
                                R_db = ac["R"]
                                od = ods[b % 2]
                                acc1 = ac["acc1"]
                                PE.waits([tacc, av_tok, const_ready])
                                tokR = PE.done(PE.e.matmul(R_db[:, 0:512], onesF[:], acc1[:], start=True, stop=True))
                                acc_state["free"][ac["ai"]] = [tokR]
                                tc_ = op(DVE, [av_tok], lambda e: e.tensor_copy(out=o12[:], in_=O_db[:, :]))
                                pp.release(O_db, [tc_])
                                t1 = op(ACT, [tokR, DVE.last], lambda e: e.activation(out=rec[:], in_=R_db[:, :], func=AF.Ln))
                                pp.release(R_db, [t1])
                                t1 = op(ACT, [t1], lambda e: e.activation(out=rec[:], in_=rec[:], func=AF.Exp, scale=-1.0))
                                t2 = op(DVE, [t1, tc_], lambda e: e.tensor_tensor(out=o12[:], in0=o12[:], in1=rec[:], op=ALU.mult))
                                op(DVE, [t2, lay_ready] + A["od_free"][b % 2], lambda e: e.scalar_tensor_tensor(out=od[:, qc, :], in0=o12[:, 512:1024], scalar=nlam, in1=o12[:, 0:512], op0=ALU.mult, op1=ALU.add))
                            else:
                                t1 = op(DVE, [av_tok, lay_ready], lambda e: e.tensor_scalar(out=rec[:, 0:512], in0=O_db[:, 512:1024], scalar1=es_t[:, qc:qc + 1], scalar2=None, op0=ALU.add))
                                t1 = op(ACT, [t1], lambda e: e.activation(out=rec[:, 0:512], in_=rec[:, 0:512], func=AF.Ln))
                                t1 = op(ACT, [t1], lambda e: e.activation(out=rec[:, 0:512], in_=rec[:, 0:512], func=AF.Exp, scale=-1.0))
                                t2 = op(DVE, [t1] + A["ot_free"][b % 2], lambda e: e.tensor_tensor(out=OT[:, qc, :], in0=O_db[:, 0:512], in1=rec[:, 0:512], op=ALU.mult))
                                pp.release(O_db, [t2])

                        pendq = []
                        DEPTH = 2
                        tacc = None
                        for (qc, k_, first, last) in items:
                            kc, qa, qb, r = k_
                            if first:
                                ac = {}
                                ac["O"], ofree = pp.alloc()
                                ac["free"] = list(ofree)
                                if lt == 0:
                                    ac["R"], rfree = pp.alloc()
                                    ac["free"] += rfree
                                    ac["ai"] = acc_state["n"] % 2
                                    acc_state["n"] += 1
                                    ac["acc1"] = accs[ac["ai"]]
                                accd[qc] = ac
                            kch = qc if lt == 0 else (0 if qc < 4 else 1)
                            S_db, sfree = pp.alloc()
                            PE.waits(sfree + [qtoks[qc]] + kv_ready)
                            ks = slice(kc * 128, (kc + 1) * 128)
                            if lt == 0:
                                PE.e.matmul(S_db[:, qa:qb], KT[0:64, kch, ks], QT[0:64, qc, qa:qb], start=True, stop=True)
                                ts_ = PE.done(PE.e.matmul(S_db[:, 512 + qa:512 + qb], KT[64:128, kch, ks], QT[64:128, qc, qa:qb], start=True, stop=True))
                            else:
                                PE.wait(tmk)
                                PE.e.matmul(S_db[:, qa:qb], KT[0:64, kch, ks], QT[0:64, qc, qa:qb], start=True, stop=False)
                                PE.e.matmul(S_db[:, 512 + qa:512 + qb], KT[64:128, kch, ks], QT[64:128, qc, qa:qb], start=True, stop=False)
                                PE.e.matmul(S_db[:, qa:qb], ident, mask[:, r, qa:qb], start=False, stop=True)
                                ts_ = PE.done(PE.e.matmul(S_db[:, 512 + qa:512 + qb], ident, mask[:, r, qa:qb], start=False, stop=True))
                            last_sc[0] = ts_
                            if len(pendq) >= DEPTH:
                                emit_av(pendq.pop(0))
                            P, pfree = pr.alloc()
                            if qa == 0 and qb == 512:
                                te = op(ACT, [ts_] + pfree, lambda e: e.activation(out=P[:, :], in_=S_db[:, :], func=AF.Exp, scale=0.125))
                            else:
                                te = op(ACT, [ts_] + pfree, lambda e: e.activation(out=P[:, :].rearrange("p (a n) -> p a n", a=2)[:, :, qa:qb], in_=S_db[:, :].rearrange("p (a n) -> p a n", a=2)[:, :, qa:qb], func=AF.Exp, scale=0.125))
                            pp.release(S_db, [te])
                            if lt == 1:
                                pendq.append((qc, k_, first, last, P, [te], None))
                            else:
                                acc1 = accd[qc]["acc1"]
                                if first:
                                    tacc = op(DVE, [te] + acc_state["free"][accd[qc]["ai"]], lambda e: e.tensor_copy(out=acc1[:], in_=P[:, 0:512]))
                                else:
                                    tacc = op(DVE, [te, tacc], lambda e: e.tensor_tensor(out=acc1[:], in0=acc1[:], in1=P[:, 0:512], op=ALU.add))
                                pendq.append((qc, k_, first, last, P, [te], tacc))
                        while pendq:
                            emit_av(pendq.pop(0))
                        return last_sc[0]

                    def subln_squares(b):
                        od = ods[b % 2]
                        toks = []
                        for qc in range(NQC):
                            toks.append(op(POOL, [DVE.last] + A["sq_free"], lambda e: e.tensor_tensor(out=sq8[:, qc, :], in0=od[:, qc, :], in1=od[:, qc, :], op=ALU.mult)))
                        return toks

                    def epi_subln(b, sqtoks):
                        od = ods[b % 2]
                        OT = OTs[b % 2]
                        tm = None
                        for pair in range(NQC // 2):
                            ps, pfree = pp.alloc()
                            PE.waits(pfree + sqtoks + [const_ready])
                            PE.e.matmul(ps[:, 0:512], onesH[:], sq8[:, 2 * pair, :], start=True, stop=True)
                            tm = PE.done(PE.e.matmul(ps[:, 512:1024], onesH[:], sq8[:, 2 * pair + 1, :], start=True, stop=True))
                            t1 = op(ACT, [tm, DVE.last], lambda e: e.activation(out=rec[:], in_=ps[:, :], func=AF.Ln, bias=eps_ap, scale=1.0))
                            pp.release(ps, [t1])
                            t2 = op(ACT, [t1], lambda e: e.activation(out=rec[:], in_=rec[:], func=AF.Exp, scale=-0.5))
                            for j_ in range(2):
                                qc = 2 * pair + j_
                                op(DVE, [t2, lay_ready] + A["ot_free"][b % 2], lambda e: e.scalar_tensor_tensor(out=OT[:, qc, :], in0=od[:, qc, :], scalar=sublnS, in1=rec[:, j_ * 512:(j_ + 1) * 512], op0=ALU.mult, op1=ALU.mult))
                        A["sq_free"] = [tm]
                        A["od_free"][b % 2] = [DVE.last, POOL.last]

                    def epi_wo(b, th_acc, part, nparts):
                        OT = OTs[b % 2]
                        hb = hbs[b % 2]
                        cs = slice(b * 512, (b + 1) * 512)
                        ot_ready = DVE.last
                        ntile = 2 if lt == 0 else 4
                        per = ntile // nparts
                        last_mm = None
                        for ti in range(part * per, (part + 1) * per):
                            wi, wt, wtok = feeder.next()
                            if lt == 0:
                                w3 = wt[:, 0:2048].rearrange("p (c n) -> p c n", c=4)
                                for mh in range(2):
                                    ps, pfree = pp.alloc()
                                    PE.waits(pfree + [wtok, ot_ready])
                                    for mm in range(2):
                                        mloc = 2 * mh + mm
                                        for c in range(4):
                                            ins = PE.e.matmul(ps[:, mm * 512:(mm + 1) * 512], w3[:, c, mloc * 128:(mloc + 1) * 128], OT[:, c, :], start=(c == 0), stop=(c == 3))
                                    tm = PE.done(ins)
                                    for mm in range(2):
                                        m = 4 * ti + 2 * mh + mm
                                        ta = op(DVE, [tm, th_acc], lambda e: e.tensor_tensor(out=hb[:, m, :], in0=ps[:, mm * 512:(mm + 1) * 512], in1=hb[:, m, :], op=ALU.add))
                                    pp.release(ps, [ta])
                                feeder.release(wi, [tm])
                            else:
                                mp = ti
                                w3 = wt[:, 0:2048].rearrange("p (c n) -> p c n", c=8)
                                ps, pfree = pp.alloc()
                                PE.waits(pfree + [wtok, ot_ready])
                                for mm in range(2):
                                    for c in range(8):
                                        ins = PE.e.matmul(ps[:, mm * 512:(mm + 1) * 512], w3[:, c, mm * 128:(mm + 1) * 128], OT[:, c, :], start=(c == 0), stop=(c == 7))
                                tm = PE.done(ins)
                                feeder.release(wi, [tm])
                                for mm in range(2):
                                    m = 2 * mp + mm
                                    ta = op(DVE, [tm, th_acc], lambda e: e.tensor_tensor(out=hb[:, m, :], in0=ps[:, mm * 512:(mm + 1) * 512], in1=hb[:, m, :], op=ALU.add))
                                pp.release(ps, [ta])
                            last_mm = tm
                        if part == nparts - 1:
                            A["ot_free"][b % 2] = [last_mm]
                            tst = dma(SP, [DVE.last], d_st[b % 2], hview(hY)[:, :, cs], hb[:])
                            A["hb_free"][b % 2] = [tst]

                    def epi_list(b, th_acc):
                        parts = []
                        if lt == 0:
                            sqtoks = subln_squares(b)
                            parts.append(lambda: epi_subln(b, sqtoks))
                            parts.append(lambda: epi_wo(b, th_acc, 0, 1))
                        else:
                            parts.append(lambda: epi_wo(b, th_acc, 0, 2))
                            parts.append(lambda: epi_wo(b, th_acc, 1, 2))
                        return parts

                    A["hb_free"] = [[DVE.last, POOL.last], [DVE.last, POOL.last]]
                    if half == halves[0] and li + 1 < L:
                        queue_casts(li + 1)
                    th = load_h(0, src)
                    tt = load_tabs(0)
                    sqt = norm_squares(hbs[0], 512, sq8, [th] + A["sq_free"], True)
                    tn = pro_norm(0, th, sqt)
                    cur = pro_q(0, tn, th, tt)
                    pending = []
                    for b in range(NB):
                        th_acc, qtoks = cur
                        nxt = {}
                        sched = [[] for _ in range(NG)]
                        epi = list(pending)
                        if lt == 0:
                            for i_, f_ in enumerate(epi):
                                sched[i_] += [f_]
                            k = 1 if epi else 0
                        else:
                            for i_, f_ in enumerate(epi):
                                sched[i_] += [f_]
                            k = max(0, len(epi) - 1)
                        if b + 1 < NB:
                            def _loads(bb=b + 1):
                                nxt["th"] = load_h(bb, src)
                                nxt["tt"] = load_tabs(bb)
                            def _squares(bb=b + 1):
                                nxt["sqt"] = norm_squares(hbs[bb % 2], 512, sq8, [nxt["th"]] + A["sq_free"], False)
                            def _norm(bb=b + 1):
                                nxt["tn"] = pro_norm(bb, nxt["th"], nxt["sqt"])
                            sched[k] += [_loads, _squares]
                            sched[k + 1] += [_norm]
                        att_last = None
                        for gi_, grp in enumerate(groups):
                            att_last = attention_group(b, grp, qtoks)
                            pop_cast()
                            for f_ in sched[gi_]:
                                f_()
                        A["qt_free"] = [att_last]
                        pending = epi_list(b, th_acc)
                        if b + 1 < NB:
                            cur = pro_q(b + 1, nxt["tn"], nxt["th"], nxt["tt"])
                    for f in pending:
                        f()
                    while cast_queue:
                        pop_cast()
                    barrier()
                    A = {"hb_free": [[], []], "hn_free": [], "qt_free": [], "ot_free": [[], []], "od_free": [[], []],
                         "tab_free": [], "sq_free": [], "kt_free": []}
                barrier()

            with ExitStack() as fsc:
                NW = n_win
                hwS = [[sb(f"hw{i}", [128, 8, 512], F32, fsc) for i in range(NW)] for _ in range(2)]
                pwS = [[sb(f"pw{i}", [128, 2, 512], BF16, fsc) for i in range(NW)] for _ in range(2)]
                hnw = [sb(f"hnw{i}", [128, 8, 512], BF16, fsc) for i in range(NW)]
                aT = [sb(f"aT{i}", [128, NFC, 512], BF16, fsc) for i in range(NW)]
                sq8f = sb("sq8f", [128, 8, 512], BF16, fsc)
                cv_items = [(sb(f"tg{i}", [128, 512], F32, fsc), sb(f"tv{i}", [128, 512], F32, fsc), sb(f"sg{i}", [128, 512], F32, fsc)) for i in range(3)]
                cvr = Fifo(cv_items)
                pl_items = [(sb(f"sgm{i}", [128, 512], F32, fsc), sb(f"ptmp{i}", [128, 512], F32, fsc)) for i in range(2)]
                plr = Fifo(pl_items)
                dst = dr["outT"] if is_last else hX
                wl = window_list(S)
                passes = [wl[p0:p0 + NW] for p0 in range(0, len(wl), NW)]
                hw_free = [[[] for _ in range(NW)] for _ in range(2)]
                pw_free = [[[] for _ in range(NW)] for _ in range(2)]
                hn_free = [[] for _ in range(NW)]
                a_free = [[] for _ in range(NW)]
                B = {"sq_free": []}
                cw = lambda t, fc: spl[:, 32 + 44 * t + fc: 32 + 44 * t + fc + 1]
                cbv = lambda fc: spl[:, 164 + fc: 165 + fc]

                def issue_loads(pi):
                    st_ = pi % 2
                    h_ready = []
                    p_ready = []
                    for wi_, (s_, e_) in enumerate(passes[pi]):
                        W = e_ - s_
                        hwt = hwS[st_][wi_]
                        lo = s_ - 1
                        hi = e_ + 1
                        deps = list(hw_free[st_][wi_])
                        toks = []
                        c0 = 0
                        if lo < 0:
                            toks.append(op(DVE, deps, lambda e: e.memset(hwt[:, :, 0:1], 0.0)))
                            lo = 0
                            c0 = 1
                        if hi > S:
                            toks.append(op(DVE, deps, lambda e: e.memset(hwt[:, :, W + 1:W + 2], 0.0)))
                            hi = S
                        toks.append(dma(SP, deps, d_h[wi_], hwt[:, :, c0:c0 + (hi - lo)], hview(hY)[:, :, lo:hi]))
                        p_ready.append(dma(POOL, pw_free[st_][wi_], d_p[wi_], pwS[st_][wi_][:, :, 0:W], dr["pT"][li].rearrange("(c p) t -> p c t", p=128)[:, :, s_:e_]))
                        h_ready.append(toks)
                    return h_ready, p_ready

                loads = issue_loads(0)
                for pi, wins in enumerate(passes):
                    st_ = pi % 2
                    hw = hwS[st_]
                    pw = pwS[st_]
                    nw = len(wins)
                    Wd = [e_ - s_ for (s_, e_) in wins]
                    h_ready, p_ready = loads
                    hn_ready = []
                    for wi_ in range(nw):
                        sqt = norm_squares(hw[wi_], Wd[wi_] + 2, sq8f, h_ready[wi_] + B["sq_free"], True)
                        tn, tmm = norm_finish(hw[wi_], Wd[wi_] + 2, 8, spl, hnw[wi_], sq8f, sqt, h_ready[wi_], hn_free[wi_])
                        B["sq_free"] = [tmm]
                        hn_ready.append(tn)
                    a_tok = [None] * nw
                    for jf in range(NFC):
                        fi, wt, wtok = feeder.next()
                        w3 = wt[:, 0:2048].rearrange("p (c n) -> p c n", c=8)
                        tm = None
                        for wi_ in range(nw):
                            W = Wd[wi_]
                            ps, pfree = pp.alloc()
                            PE.waits(pfree + [wtok, hn_ready[wi_]])
                            for c in range(8):
                                PE.e.matmul(ps[:, 0:W + 2], w3[:, c, 0:128], hnw[wi_][:, c, 0:W + 2], start=(c == 0), stop=(c == 7))
                            for c in range(8):
                                ins = PE.e.matmul(ps[:, 512:512 + W + 2], w3[:, c, 128:256], hnw[wi_][:, c, 0:W + 2], start=(c == 0), stop=(c == 7))
                            tm = PE.done(ins)
                            (tg, tv, sg), cfree = cvr.alloc()
                            fg, fv = jf, NFC + jf
                            g1 = op(ACT, [tm] + cfree, lambda e: e.activation(out=tg[:, 0:W], in_=ps[:, 0:W], func=AF.Identity, scale=cw(0, fg), bias=cbv(fg)))
                            v1 = op(ACT, [tm] + cfree, lambda e: e.activation(out=tv[:, 0:W], in_=ps[:, 512:512 + W], func=AF.Identity, scale=cw(0, fv), bias=cbv(fv)))
                            g2 = op(DVE, [g1], lambda e: e.scalar_tensor_tensor(out=tg[:, 0:W], in0=ps[:, 1:W + 1], scalar=cw(1, fg), in1=tg[:, 0:W], op0=ALU.mult, op1=ALU.add))
                            v2 = op(DVE, [v1], lambda e: e.scalar_tensor_tensor(out=tv[:, 0:W], in0=ps[:, 513:513 + W], scalar=cw(1, fv), in1=tv[:, 0:W], op0=ALU.mult, op1=ALU.add))
                            g3 = op(DVE, [g2], lambda e: e.scalar_tensor_tensor(out=tg[:, 0:W], in0=ps[:, 2:W + 2], scalar=cw(2, fg), in1=tg[:, 0:W], op0=ALU.mult, op1=ALU.add))
                            v3 = op(DVE, [v2], lambda e: e.scalar_tensor_tensor(out=tv[:, 0:W], in0=ps[:, 514:514 + W], scalar=cw(2, fv), in1=tv[:, 0:W], op0=ALU.mult, op1=ALU.add))
                            pp.release(ps, [g3, v3])
                            s1 = op(ACT, [g3], lambda e: e.activation(out=sg[:, 0:W], in_=tg[:, 0:W], func=AF.Silu))
                            a_tok[wi_] = op(POOL, [s1, v3] + a_free[wi_], lambda e: e.tensor_tensor(out=aT[wi_][:, jf, 0:W], in0=sg[:, 0:W], in1=tv[:, 0:W], op=ALU.mult))
                            cvr.release((tg, tv, sg), [a_tok[wi_]])
                        feeder.release(fi, [tm])
                    loads_next = issue_loads(pi + 1) if pi + 1 < len(passes) else None
                    h2_tok = [None] * nw
                    for m in range(8):
                        dbs = [None] * nw
                        for half in range(2):
                            fi, wt, wtok = feeder.next()
                            w3 = wt[:, 0:1408].rearrange("p (c n) -> p c n", c=11)
                            tm = None
                            for wi_ in range(nw):
                                W = Wd[wi_]
                                if half == 0:
                                    ps, pfree = pp.alloc()
                                    dbs[wi_] = ps
                                    PE.waits(pfree)
                                ps = dbs[wi_]
                                PE.waits([wtok, a_tok[wi_]])
                                for kk in range(11):
                                    ins = PE.e.matmul(ps[:, 0:W], w3[:, kk, :], aT[wi_][:, half * 11 + kk, 0:W], start=(half == 0 and kk == 0), stop=(half == 1 and kk == 10))
                                tm = PE.done(ins)
                                if half == 1:
                                    h2_tok[wi_] = op(DVE, [tm] + h_ready[wi_] + [hn_ready[wi_]], lambda e: e.tensor_tensor(out=hw[wi_][:, m, 1:W + 1], in0=ps[:, 0:W], in1=hw[wi_][:, m, 1:W + 1], op=ALU.add))
                                    pp.release(ps, [h2_tok[wi_]])
                                    a_free[wi_] = [tm]
                            feeder.release(fi, [tm])
                    hn2_ready = []
                    for wi_ in range(nw):
                        sqt = norm_squares(hw[wi_], Wd[wi_], sq8f, [h2_tok[wi_]] + B["sq_free"], True, off=1)
                        tn, tmm = norm_finish(hw[wi_], Wd[wi_], 16, spl, hnw[wi_], sq8f, sqt, [h2_tok[wi_]], [PE.last], off=1)
                        B["sq_free"] = [tmm]
                        hn2_ready.append(tn)
                    gate_tiles = [feeder.next() for _ in range(4)]
                    pj_i, pj_t, pj_tok = feeder.next()
                    pj3 = pj_t[:, 0:2048].rearrange("p (c n) -> p c n", c=2)
                    h3_tok = [None] * nw
                    tm = None
                    for mp in range(4):
                        fi, wt, wtok = gate_tiles[mp]
                        w3 = wt[:, 0:2048].rearrange("p (c n) -> p c n", c=8)
                        for mm in range(2):
                            m = 2 * mp + mm
                            for wi_ in range(nw):
                                W = Wd[wi_]
                                ps, pfree = pp.alloc()
                                PE.waits(pfree + [wtok, pj_tok, hn2_ready[wi_], p_ready[wi_]])
                                for c in range(8):
                                    PE.e.matmul(ps[:, 0:W], w3[:, c, mm * 128:(mm + 1) * 128], hnw[wi_][:, c, 0:W], start=(c == 0), stop=(c == 7))
                                for c in range(2):
                                    ins = PE.e.matmul(ps[:, 512:512 + W], pj3[:, c, m * 128:(m + 1) * 128], pw[wi_][:, c, 0:W], start=(c == 0), stop=(c == 1))
                                tm = PE.done(ins)
                                (sgm, ptmp), lfree = plr.alloc()
                                s1 = op(ACT, [tm] + lfree, lambda e: e.activation(out=sgm[:, 0:W], in_=ps[:, 0:W], func=AF.Sigmoid))
                                s2 = op(DVE, [s1], lambda e: e.tensor_tensor(out=ptmp[:, 0:W], in0=ps[:, 512:512 + W], in1=sgm[:, 0:W], op=ALU.mult))
                                pp.release(ps, [s2])
                                h3_tok[wi_] = op(POOL, [s2, hn2_ready[wi_]], lambda e: e.tensor_tensor(out=hw[wi_][:, m, 1:W + 1], in0=hw[wi_][:, m, 1:W + 1], in1=ptmp[:, 0:W], op=ALU.add))
                                plr.release((sgm, ptmp), [h3_tok[wi_]])
                        feeder.release(fi, [tm])
                    feeder.release(pj_i, [tm])
                    for wi_, (s_, e_) in enumerate(wins):
                        W = Wd[wi_]
                        tk = h3_tok[wi_]
                        if is_last and apply_final:
                            tk = norm_block(hw[wi_], W, 24, spl, None, [tk], [], off=1, out_f32=hw[wi_])
                        tst = dma(SP, [tk, POOL.last], d_st[wi_], hview(dst)[:, :, s_:e_], hw[wi_][:, :, 1:W + 1])
                        hw_free[st_][wi_] = [tst]
                        pw_free[st_][wi_] = [PE.last]
                        hn_free[wi_] = [PE.last]
                    loads = loads_next
                barrier()
        SP.wait(d_st[0].tok())
        SP.wait(d_st[1].tok())
    return nc


LAYER_TYPES = [0, 1, 0, 1]
_PROGRAM_CACHE = {}


def _get_program(S, layers, apply_final, n_win):
    key = (S, tuple(layers), apply_final, n_win)
    if key not in _PROGRAM_CACHE:
        _PROGRAM_CACHE[key] = build_program(S, list(layers), apply_final, n_win)
    return _PROGRAM_CACHE[key]


def run_layers(xT_list, inp, layer_ids, pos_list, apply_final, n_win=2, core_ids=None):
    S = xT_list[0].shape[1]
    layers = [(gi, LAYER_TYPES[gi]) for gi in layer_ids]
    nc = _get_program(S, layers, apply_final, n_win)
    cst, msk = host_consts()
    packs = [pack_layer(LAYER_TYPES[gi], gi // 2, inp, gi) for gi in layer_ids]
    in_maps = []
    ncore = len(xT_list)
    for ci in range(ncore):
        m = {"xT": np.ascontiguousarray(xT_list[ci], dtype=np.float32), "pos": pos_list[ci], "cst": cst, "msk": msk, "perm": host_perm()}
        m["pT"] = np.ascontiguousarray(np.stack([inp["p"][gi][ci].T for gi in layer_ids]), dtype=np.float32)
        for li, pk in enumerate(packs):
            m[f"wkv{li}"] = pk["wkv"]
            m[f"wa{li}"] = pk["wa"]
            m[f"wb{li}"] = pk["wb"]
            m[f"sp{li}"] = pk["sp"]
        in_maps.append(m)
    res = run_bass_kernel_spmd(nc, in_maps, core_ids=list(range(ncore)) if core_ids is None else core_ids)
    return [r["outT"] for r in res.results]


FUSED = True


def kernel(**inputs):
    inp = {k: np.asarray(v) for k, v in inputs.items()}
    x = inp["x"]
    B, S, _ = x.shape
    xT = [np.ascontiguousarray(x[b].T) for b in range(B)]
    pos = [np.ascontiguousarray(inp["positions"][b].reshape(1, S).astype(np.int32)) for b in range(B)]
    if FUSED:
        outs = run_layers(xT, inp, [0, 1, 2, 3], pos, True)
    else:
        cur = xT
        for gi in range(4):
            cur = run_layers(cur, inp, [gi], pos, gi == 3)
        outs = cur
    return np.stack([o.T for o in outs]).astype(np.float32)
```

```python
import math
from contextlib import ExitStack

import numpy as np
import concourse.bass as bass
import concourse.mybir as mybir
from concourse.bass_utils import run_bass_kernel_spmd

F32 = mybir.dt.float32
BF16 = mybir.dt.bfloat16
I32 = mybir.dt.int32
AF = mybir.ActivationFunctionType
ALU = mybir.AluOpType
AX = mybir.AxisListType

D = 1024
DFF = 2816
NFC = 22
PLE = 256
EPS = 1e-6
TE = 2048
NSP = 480
NSLOT = 5
TWO_PI = 2.0 * math.pi
C1 = 6.28125
C2 = TWO_PI - C1


def _tile_kc(W, cols, nk=None):
    K = W.shape[0]
    kc = K // 128
    t = W[:, cols].reshape(kc, 128, len(cols)).transpose(1, 0, 2).reshape(128, -1)
    out = np.zeros((128, TE), np.float32)
    out[:, : t.shape[1]] = t
    return out


def _rot_cols(cols):
    cols = np.asarray(cols)
    r = cols.reshape(-1, 2, 32)[:, ::-1, :].reshape(-1)
    return r


def swa_chunk_heads(c):
    return (c, 4 + c) if c < 4 else (8 + c - 4, 12 + c - 4)


def pack_layer(lt, j, inp, i):
    out = {}
    ar = np.arange
    if lt == 0:
        wqkv = inp["diff_w_qkv"][j]
        wo = inp["diff_w_o"][j]
        kv = []
        for half in range(2):
            for c in range(4 * half, 4 * half + 4, 2):
                kv.append(_tile_kc(wqkv, 1024 + c * 128 + ar(256)))
            for vt in range(2 * half, 2 * half + 2):
                kv.append(_tile_kc(wqkv, 2048 + vt * 256 + ar(256)))
        a = []
        for half in range(2):
            for c in range(4 * half, 4 * half + 4, 2):
                a.append(_tile_kc(wqkv, c * 128 + ar(256)))
            wo_h = wo[half * 512:(half + 1) * 512]
            for m in range(8):
                a.append(_tile_kc(wo_h, m * 128 + ar(128)))
    else:
        wqkv = inp["swa_w_qkv"][j]
        wo = inp["swa_w_o"][j]
        kv = []
        kv.append(_tile_kc(wqkv, 1024 + ar(256)))
        kv.append(_tile_kc(wqkv, 1280 + ar(256)))
        a = []
        rowperm = []
        qcols = []
        for c in range(8):
            hA, hB = swa_chunk_heads(c)
            cols = np.concatenate([hA * 64 + ar(64), hB * 64 + ar(64)])
            rowperm.append(cols)
            qcols.append(cols)
        for c in range(0, 8, 2):
            a.append(_tile_kc(wqkv, np.concatenate([qcols[c], qcols[c + 1]])))
        wo_p = wo[np.concatenate(rowperm)]
        for mp in range(4):
            a.append(_tile_kc(wo_p, mp * 256 + ar(256)))
    out["wkv"] = np.stack(kv)
    out["wa"] = np.stack(a)
    b = []
    wup = inp["ffn_w_up"][i]
    for jf in range(NFC):
        b.append(_tile_kc(wup, np.concatenate([jf * 128 + ar(128), DFF + jf * 128 + ar(128)])))
    wdn = inp["ffn_w_down"][i]
    for m in range(8):
        for half in range(2):
            b.append(_tile_kc(wdn[half * 11 * 128:(half + 1) * 11 * 128], m * 128 + ar(128)))
    wg = inp["ple_w_gate"][i]
    for mp in range(4):
        b.append(_tile_kc(wg, mp * 256 + ar(256)))
    b.append(_tile_kc(inp["ple_w_proj"][i], ar(1024)))
    out["wb"] = np.stack(b)
    sp = np.zeros((128, NSP), np.float32)
    sp[:, 0:8] = inp["attn_norm"][i].reshape(8, 128).T
    sp[:, 8:16] = inp["ffn_norm"][i].reshape(8, 128).T
    sp[:, 16:24] = inp["ple_norm"][i].reshape(8, 128).T
    sp[:, 24:32] = inp["final_norm"].reshape(8, 128).T
    cw = inp["ffn_conv_w"][i]
    for t in range(3):
        sp[:, 32 + 44 * t: 32 + 44 * (t + 1)] = cw[t].reshape(44, 128).T
    sp[:, 164:208] = inp["ffn_conv_b"][i].reshape(44, 128).T
    if lt == 0:
        sp[:, 208] = inp["diff_subln"][j]
        sp[:, 217:473] = inp["diff_lambda"][j].reshape(1, 256)
    else:
        sk = inp["swa_sinks"][j]
        for c in range(8):
            hA, hB = swa_chunk_heads(c)
            sp[0:64, 209 + c] = sk[hA]
            sp[64:128, 209 + c] = sk[hB]
    out["sp"] = sp
    return out


def host_consts():
    cst = np.zeros((128, 8), np.float32)
    inv_freq = (10000.0 ** (-np.arange(0, 64, 2, dtype=np.float32) / np.float32(64))).astype(np.float32)
    p = np.arange(128)
    cst[:, 0] = inv_freq[p % 32]
    cst[:, 1] = np.where((p % 64) < 32, -1.0, 1.0)
    cst[:, 2] = EPS
    kk = np.arange(128)[:, None]
    qq = np.arange(512)[None, :]
    masks = np.zeros((128, 6, 512), np.float32)
    for r in range(6):
        rel = r - 1
        masks[:, r, :] = np.where(np.abs(qq - kk - 128 * rel) <= 128, 0.0, -30000.0).astype(np.float32)
    ident = np.eye(128, dtype=np.float32)
    return cst, np.concatenate([masks.reshape(128, 6 * 512), ident], axis=1)


def host_perm():
    p = np.arange(128)
    partner = np.where((p % 64) < 32, p + 32, p - 32)
    pm = np.zeros((128, 128), np.float32)
    pm[partner, p] = 1.0
    return pm


class Eng:
    def __init__(self, nc, es, eng, name, own_sem=True):
        self.e = eng
        self.name = name
        self.sem = es.enter_context(nc.semaphore("s_" + name)) if own_sem else None
        self.cnt = 0
        self.waited = {}
        self.last = None

    def wait(self, tok):
        if tok is None:
            return
        key, sem, val = tok
        if self.waited.get(key, 0) >= val:
            return
        self.e.wait_ge(sem, val)
        self.waited[key] = val

    def waits(self, toks):
        for t in toks:
            self.wait(t)

    def done(self, ins):
        ins.then_inc(self.sem, 1)
        self.cnt += 1
        self.last = (self.name, self.sem, self.cnt)
        return self.last


class DSem:
    def __init__(self, nc, es, name, registry):
        self.sem = es.enter_context(nc.semaphore("d_" + name))
        self.name = "d_" + name
        self.cnt = 0
        registry.append(self)

    def tok(self):
        return (self.name, self.sem, self.cnt)

    def fire(self, ins):
        ins.then_inc(self.sem, 16)
        self.cnt += 16
        return self.tok()


class Fifo:
    def __init__(self, items):
        self.free = [(it, []) for it in items]

    def alloc(self):
        assert self.free, "pool exhausted"
        return self.free.pop(0)

    def release(self, item, toks):
        self.free.append((item, list(toks)))


def window_list(S):
    n = -(-S // 510)
    base = S // n
    rem = S - base * n
    ws = []
    s = 0
    for i in range(n):
        w = base + (1 if i < rem else 0)
        ws.append((s, s + w))
        s += w
    assert s == S
    return ws


def build_program(S, layers, apply_final, n_win=2):
    nc = bass.Bass("TRN2", target_bir_lowering=False)
    L = len(layers)
    NT = S // 128
    NB = S // 512
    dr = {}
    dr["xT"] = nc.dram_tensor("xT", [D, S], F32, kind="ExternalInput").ap()
    dr["pT"] = nc.dram_tensor("pT", [L, PLE, S], F32, kind="ExternalInput").ap()
    dr["pos"] = nc.dram_tensor("pos", [1, S], I32, kind="ExternalInput").ap()
    dr["cst"] = nc.dram_tensor("cst", [128, 8], F32, kind="ExternalInput").ap()
    dr["msk"] = nc.dram_tensor("msk", [128, 6 * 512 + 128], F32, kind="ExternalInput").ap()
    dr["perm"] = nc.dram_tensor("perm", [128, 128], F32, kind="ExternalInput").ap()
    dr["outT"] = nc.dram_tensor("outT", [D, S], F32, kind="ExternalOutput").ap()
    hX = nc.dram_tensor("hX", [D, S], F32, kind="Internal").ap()
    hY = nc.dram_tensor("hY", [D, S], F32, kind="Internal").ap()
    tabC = nc.dram_tensor("tabC", [128, S], F32, kind="Internal").ap()
    tabS = nc.dram_tensor("tabS", [128, S], F32, kind="Internal").ap()
    wsrc, wbf = [], []
    for li, (gi, lt) in enumerate(layers):
        nkv = 8 if lt == 0 else 2
        na = 20 if lt == 0 else 8
        d = {}
        e = {}
        for nm, n in (("wkv", nkv), ("wa", na), ("wb", 43)):
            d[nm] = nc.dram_tensor(f"{nm}{li}", [n, 128, TE], F32, kind="ExternalInput").ap()
            e[nm] = nc.dram_tensor(f"{nm}b{li}", [n, 128, TE], BF16, kind="Internal").ap()
        d["sp"] = nc.dram_tensor(f"sp{li}", [128, NSP], F32, kind="ExternalInput").ap()
        wsrc.append(d)
        wbf.append(e)

    es = ExitStack()
    with es:
        dsems = []
        PE = Eng(nc, es, nc.tensor, "pe")
        ACT = Eng(nc, es, nc.scalar, "act")
        DVE = Eng(nc, es, nc.vector, "dve")
        POOL = Eng(nc, es, nc.gpsimd, "pool")
        SP = Eng(nc, es, nc.sync, "sp", own_sem=False)
        ENGS = [PE, ACT, DVE, POOL]

        uid = [0]

        def sb(name, shape, dt, stack=es):
            uid[0] += 1
            return stack.enter_context(nc.sbuf_tensor(f"{name}_{uid[0]}", shape, dt))

        def op(E, deps, fn):
            E.waits(deps)
            return E.done(fn(E.e))

        def dma(Q, deps, ds, out, in_):
            Q.waits(deps)
            return ds.fire(Q.e.dma_start(out=out, in_=in_))

        def barrier():
            toks = [E.last for E in ENGS if E.last is not None] + [d.tok() for d in dsems if d.cnt > 0 and not d.name.startswith('d_cast')]
            for E in ENGS + [SP]:
                E.waits(toks)

        cst = sb("cst", [128, 8], F32)
        spt = [sb(f"spt{li}", [128, NSP], F32) for li in range(L)]
        ones = sb("ones", [128, 128], BF16)
        onesD = sb("onesD", [128, 128], BF16)
        onesH = sb("onesH", [128, 128], BF16)
        onesF = sb("onesF", [128, 128], F32)
        permT = sb("permT", [128, 128], BF16)
        wring = [sb(f"wr{i}", [128, TE], BF16) for i in range(NSLOT)]
        sq_items = [sb(f"sq{i}", [128, 512], BF16) for i in range(3)]
        rs = sb("rs", [128, 512], F32)
        es_t = sb("es_t", [128, 8], F32)
        lam_t = sb("lam_t", [128, 8], F32)
        lamw = sb("lamw", [128, 128], F32)
        psum = [es.enter_context(nc.psum_tensor(f"ps{i}", [128, 1024], F32)) for i in range(4)]
        pp = Fifo(psum)
        sqr = Fifo(sq_items)

        d_misc = DSem(nc, es, "misc", dsems)
        d_miscp = DSem(nc, es, "miscp", dsems)
        d_cast = [[DSem(nc, es, f"cast{li}_{k}", dsems) for k in range(3)] for li in range(L)]
        d_w = [DSem(nc, es, f"w{i}", dsems) for i in range(NSLOT)]
        d_h = [DSem(nc, es, f"h{i}", dsems) for i in range(2)]
        d_tab = DSem(nc, es, "tab", dsems)
        d_st = [DSem(nc, es, f"st{i}", dsems) for i in range(2)]
        d_p = [DSem(nc, es, f"p{i}", dsems) for i in range(2)]

        t_c = dma(SP, [], d_misc, cst[:], dr["cst"])
        perm_ready = dma(POOL, [], d_miscp, permT[:], dr["perm"])
        for li in range(L):
            t_c = dma(SP, [], d_misc, spt[li][:], wsrc[li]["sp"])
        cast_tok = []
        cast_plan = []
        for li in range(L):
            toks = []
            plans = []
            for k, nm in enumerate(("wkv", "wa", "wb")):
                n = wsrc[li][nm].shape[0]
                src_ = wsrc[li][nm].rearrange("n p f -> (n p) f")
                dst_ = wbf[li][nm].rearrange("n p f -> (n p) f")
                i0 = 0
                pl = []
                while i0 < n:
                    i1 = min(n, i0 + 8)
                    pl.append((dst_[i0 * 128:i1 * 128, :], src_[i0 * 128:i1 * 128, :]))
                    i0 = i1
                plans.append(pl)
                toks.append((d_cast[li][k].name, d_cast[li][k].sem, 16 * len(pl)))
            cast_tok.append(toks)
            cast_plan.append(plans)

        def issue_casts(li_):
            for k in range(3):
                for (dd, ss) in cast_plan[li_][k]:
                    dma(POOL, [], d_cast[li_][k], dd, ss)

        cast_queue = []

        def queue_casts(li_):
            for k in range(3):
                for (dd, ss) in cast_plan[li_][k]:
                    cast_queue.append((d_cast[li_][k], dd, ss))

        def pop_cast():
            if cast_queue:
                ds_, dd, ss = cast_queue.pop(0)
                dma(POOL, [], ds_, dd, ss)

        issue_casts(0)
        c_ready = t_c
        t0 = op(DVE, [], lambda e: e.memset(ones[:], 1.0))
        t0 = op(DVE, [], lambda e: e.memset(onesD[:], 1.0 / 1024))
        t0 = op(DVE, [], lambda e: e.memset(onesF[:], 1.0))
        const_ready = op(DVE, [], lambda e: e.memset(onesH[:], 1.0 / 128))

        invf = cst[:, 0:1]
        sgn = cst[:, 1:2]
        eps_ap = cst[:, 2:3]
        with ExitStack() as ts:
            posi = sb("posi", [128, S], I32, ts)
            tA = sb("tA", [128, S], F32, ts)
            tB = sb("tB", [128, S], F32, ts)
            tK = sb("tK", [128, S], F32, ts)
            tR = sb("tR", [128, S], F32, ts)
            tO = [sb(f"tO{i}", [128, S], F32, ts) for i in range(2)]
            ki = sb("ki", [128, S], I32, ts)
            last_tab = None
            tl = dma(SP, [], d_tab, posi[:], dr["pos"][0:1, :].partition_broadcast(128))
            t = op(DVE, [tl, c_ready], lambda e: e.tensor_copy(out=tA[:], in_=posi[:]))
            t = op(DVE, [t], lambda e: e.tensor_scalar(out=tA[:], in0=tA[:], scalar1=invf, scalar2=None, op0=ALU.mult))
            for which in range(2):
                if which == 1:
                    t = op(DVE, [t], lambda e: e.tensor_scalar(out=tA[:], in0=tA[:], scalar1=float(math.pi / 2), scalar2=None, op0=ALU.add))
                t = op(DVE, [t], lambda e: e.tensor_scalar(out=tB[:], in0=tA[:], scalar1=float(1.0 / TWO_PI), scalar2=None, op0=ALU.mult))
                t = op(DVE, [t], lambda e: e.tensor_copy(out=ki[:], in_=tB[:]))
                t = op(DVE, [t], lambda e: e.tensor_copy(out=tK[:], in_=ki[:]))
                t = op(DVE, [t, ACT.last], lambda e: e.scalar_tensor_tensor(out=tR[:], in0=tK[:], scalar=-C1, in1=tA[:], op0=ALU.mult, op1=ALU.add))
                t = op(DVE, [t], lambda e: e.scalar_tensor_tensor(out=tR[:], in0=tK[:], scalar=-C2, in1=tR[:], op0=ALU.mult, op1=ALU.add))
                t = op(DVE, [t], lambda e: e.tensor_scalar(out=tB[:], in0=tR[:], scalar1=float(math.pi), scalar2=float(-TWO_PI), op0=ALU.is_gt, op1=ALU.mult))
                t = op(DVE, [t], lambda e: e.tensor_tensor(out=tR[:], in0=tR[:], in1=tB[:], op=ALU.add))
                t = op(DVE, [t], lambda e: e.tensor_scalar(out=tR[:], in0=tR[:], scalar1=float(math.pi), scalar2=float(-math.pi), op0=ALU.min, op1=ALU.max))
                to = tO[which]
                ta = op(ACT, [t], lambda e: e.activation(out=to[:], in_=tR[:], func=AF.Sin))
                if which == 0:
                    ta = op(DVE, [ta], lambda e: e.tensor_scalar(out=to[:], in0=to[:], scalar1=sgn, scalar2=None, op0=ALU.mult))
                t = ta
                last_tab = dma(SP, [ta], d_tab, (tabS if which == 0 else tabC)[:, :], to[:])
            barrier()
        tab_ready = last_tab

        class Feeder:
            def __init__(self):
                self.seq = []
                self.issued = 0
                self.pos = 0
                self.rel = {}
                self.ready = {}

            def plan(self, tiles):
                self.seq.extend(tiles)

            def _issue(self, i):
                ap, ctok = self.seq[i]
                deps = [ctok]
                if i >= NSLOT:
                    deps += self.rel[i - NSLOT]
                self.ready[i] = dma(SP, deps, d_w[i % NSLOT], wring[i % NSLOT][:], ap)
                self.issued = i + 1

            def start(self):
                for i in range(min(NSLOT, len(self.seq))):
                    self._issue(i)

            def next(self):
                i = self.pos
                self.pos += 1
                return i, wring[i % NSLOT], self.ready[i]

            def release(self, i, toks):
                self.rel[i] = list(toks)
                if i + NSLOT < len(self.seq):
                    self._issue(i + NSLOT)

        feeder = Feeder()

        def hview(ap):
            return ap.rearrange("(c p) t -> p c t", p=128)

        def norm_block(h, W, g0, spl, hn, deps_h, deps_hn, off=0, out_f32=None):
            ps, pfree = pp.alloc()
            PE.waits(pfree)
            tmm = None
            for c in range(8):
                sq, sfree = sqr.alloc()
                tsq = op(POOL, deps_h + sfree, lambda e: e.tensor_tensor(out=sq[:, 0:W], in0=h[:, c, off:off + W], in1=h[:, c, off:off + W], op=ALU.mult))
                PE.wait(tsq)
                PE.wait(const_ready)
                tmm = PE.done(PE.e.matmul(ps[:, 0:W], onesD[:], sq[:, 0:W], start=(c == 0), stop=(c == 7)))
                sqr.release(sq, [tmm])
            t1 = op(ACT, [tmm, DVE.last], lambda e: e.activation(out=rs[:, 0:W], in_=ps[:, 0:W], func=AF.Ln, bias=eps_ap, scale=1.0))
            pp.release(ps, [t1])
            t2 = op(ACT, [t1], lambda e: e.activation(out=rs[:, 0:W], in_=rs[:, 0:W], func=AF.Exp, scale=-0.5))
            t = None
            for c in range(8):
                if out_f32 is None:
                    t = op(DVE, [t2] + deps_h + deps_hn, lambda e: e.scalar_tensor_tensor(out=hn[:, c, 0:W], in0=h[:, c, off:off + W], scalar=spl[:, g0 + c:g0 + c + 1], in1=rs[:, 0:W], op0=ALU.mult, op1=ALU.mult))
                else:
                    t = op(DVE, [t2] + deps_h + deps_hn, lambda e: e.scalar_tensor_tensor(out=out_f32[:, c, off:off + W], in0=h[:, c, off:off + W], scalar=spl[:, g0 + c:g0 + c + 1], in1=rs[:, 0:W], op0=ALU.mult, op1=ALU.mult))
            return t

        def norm_squares(h, W, sq8, deps, use_act, off=0):
            toks = []
            for c in range(8):
                if use_act and c % 2 == 1:
                    toks.append(op(ACT, deps, lambda e: e.activation(out=sq8[:, c, 0:W], in_=h[:, c, off:off + W], func=AF.Square)))
                else:
                    toks.append(op(POOL, deps, lambda e: e.tensor_tensor(out=sq8[:, c, 0:W], in0=h[:, c, off:off + W], in1=h[:, c, off:off + W], op=ALU.mult)))
            return toks

        def norm_finish(h, W, g0, spl, hn, sq8, sq_toks, deps_h, deps_hn, off=0):
            ps, pfree = pp.alloc()
            PE.waits(pfree + sq_toks + [const_ready])
            ins = None
            for c in range(8):
                ins = PE.e.matmul(ps[:, 0:W], onesD[:], sq8[:, c, 0:W], start=(c == 0), stop=(c == 7))
            tmm = PE.done(ins)
            t1 = op(ACT, [tmm, DVE.last], lambda e: e.activation(out=rs[:, 0:W], in_=ps[:, 0:W], func=AF.Ln, bias=eps_ap, scale=1.0))
            pp.release(ps, [t1])
            t2 = op(ACT, [t1], lambda e: e.activation(out=rs[:, 0:W], in_=rs[:, 0:W], func=AF.Exp, scale=-0.5))
            t = None
            for c in range(8):
                t = op(DVE, [t2] + deps_h + deps_hn, lambda e: e.scalar_tensor_tensor(out=hn[:, c, 0:W], in0=h[:, c, off:off + W], scalar=spl[:, g0 + c:g0 + c + 1], in1=rs[:, 0:W], op0=ALU.mult, op1=ALU.mult))
            return t, tmm

        steps = []

        def tiles_of(li, nm, idxs):
            k = {"wkv": 0, "wa": 1, "wb": 2}[nm]
            return [(wbf[li][nm][i], cast_tok[li][k]) for i in idxs]

        for li, (gi, lt) in enumerate(layers):
            if lt == 0:
                for half in range(2):
                    for b in range(NB):
                        feeder.plan(tiles_of(li, "wkv", range(4 * half, 4 * half + 4)))
                    qt_ = list(range(10 * half, 10 * half + 2))
                    wo_ = list(range(10 * half + 2, 10 * half + 10))
                    feeder.plan(tiles_of(li, "wa", qt_))
                    for b in range(NB):
                        if b + 1 < NB:
                            feeder.plan(tiles_of(li, "wa", qt_))
                        feeder.plan(tiles_of(li, "wa", wo_))
            else:
                for b in range(NB):
                    feeder.plan(tiles_of(li, "wkv", range(2)))
                feeder.plan(tiles_of(li, "wa", range(4)))
                for b in range(NB):
                    if b + 1 < NB:
                        feeder.plan(tiles_of(li, "wa", range(4)))
                    feeder.plan(tiles_of(li, "wa", range(4, 8)))
            wl = window_list(S)
            for p0 in range(0, len(wl), n_win):
                feeder.plan(tiles_of(li, "wb", range(43)))
        feeder.start()

        for li, (gi, lt) in enumerate(layers):
            spl = spt[li]
            src = dr["xT"] if li == 0 else hX
            is_last = (li == L - 1)
            lam_init = 0.8 - 0.6 * math.exp(-0.3 * gi)
            halves = [0, 1] if lt == 0 else [0]
            NQC = 4 if lt == 0 else 8
            NKC = 4 if lt == 0 else 2
            VW = 512 if lt == 0 else 256
            NVT = VW // 256

            if lt == 0:
                l = spl[:, 217:473]
                t = op(DVE, [c_ready], lambda e: e.tensor_tensor(out=lamw[:, 0:64], in0=l[:, 0:64], in1=l[:, 64:128], op=ALU.mult))
                t = op(DVE, [c_ready], lambda e: e.tensor_tensor(out=lamw[:, 64:128], in0=l[:, 128:192], in1=l[:, 192:256], op=ALU.mult))
                t = op(DVE, [t], lambda e: e.tensor_reduce(out=lam_t[:, 0:1], in_=lamw[:, 0:64], axis=AX.X, op=ALU.add))
                t = op(DVE, [t], lambda e: e.tensor_reduce(out=lam_t[:, 1:2], in_=lamw[:, 64:128], axis=AX.X, op=ALU.add))
                t = op(ACT, [t], lambda e: e.activation(out=lam_t[:, 2:4], in_=lam_t[:, 0:2], func=AF.Exp))
                t = op(DVE, [t], lambda e: e.scalar_tensor_tensor(out=lam_t[:, 4:5], in0=lam_t[:, 3:4], scalar=float(-lam_init), in1=lam_t[:, 2:3], op0=ALU.add, op1=ALU.subtract))
                t = op(DVE, [t], lambda e: e.tensor_scalar(out=lam_t[:, 5:6], in0=spl[:, 208:209], scalar1=float(1.0 - lam_init), scalar2=None, op0=ALU.mult))
                lay_ready = t
                nlam = lam_t[:, 4:5]
                sublnS = lam_t[:, 5:6]
            else:
                lay_ready = op(ACT, [c_ready], lambda e: e.activation(out=es_t[:, 0:8], in_=spl[:, 209:217], func=AF.Exp))

            with ExitStack() as asc:
                KT = sb("KT", [128, NKC, S], BF16, asc)
                V = sb("V", [128, NT, VW], BF16, asc)
                hbs = [sb("hb", [128, 8, 512], F32, asc) for _ in range(2)]
                hn = sb("hn", [128, 8, 512], BF16, asc)
                sq8 = sb("sq8", [128, 8, 512], BF16, asc)
                QT = sb("QT", [128, NQC, 512], BF16, asc)
                OTall = sb("OT", [128, 2 * NQC, 512], BF16, asc)
                OTs = [OTall[:, 0:NQC, :], OTall[:, NQC:2 * NQC, :]]
                hnb = [hn, OTall[:, 0:8, :]]
                ctab = sb("ctab", [128, 512], F32, asc)
                stab = sb("stab", [128, 512], F32, asc)
                rt_items = [(sb(f"rt{i}a", [128, 512], F32, asc), sb(f"rt{i}b", [128, 512], F32, asc)) for i in range(2)]
                rtr = Fifo(rt_items)
                qbr = Fifo([sb(f"qb{i}", [128, 512], BF16, asc) for i in range(2)])
                p_items = [sb(f"P{i}", [128, 1024], BF16, asc) for i in range(3 if lt == 0 else 4)]
                pr = Fifo(p_items)
                rec = sb("rec", [128, 1024], F32, asc)
                o12 = sb("o12", [128, 1024], F32, asc)
                if lt == 0:
                    ods = [sb("od", [128, NQC, 512], F32, asc) for _ in range(2)]
                    accs = [sb("acc", [128, 512], F32, asc) for _ in range(2)]
                    acc_state = {"n": 0, "free": [[], []]}
                else:
                    maskf = sb("mask", [128, 6 * 512 + 128], BF16, asc)
                    tmk = dma(POOL, [], d_miscp, maskf[:], dr["msk"])
                    mask = maskf[:, 0:3072].rearrange("p (r q) -> p r q", r=6)
                    ident = maskf[:, 3072:3200]

                A = {"hb_free": [[], []], "hn_free": [], "qt_free": [], "ot_free": [[], []], "od_free": [[], []],
                     "tab_free": [], "sq_free": [], "kt_free": []}

                def load_h(b, srcap):
                    cs = slice(b * 512, (b + 1) * 512)
                    return dma(SP, A["hb_free"][b % 2], d_h[b % 2], hbs[b % 2][:], hview(srcap)[:, :, cs])

                def load_tabs(b):
                    cs = slice(b * 512, (b + 1) * 512)
                    dma(SP, A["tab_free"] + [tab_ready], d_tab, ctab[:], tabC[:, cs])
                    return dma(SP, [], d_tab, stab[:], tabS[:, cs])

                def proj_rope(wt, wtok, hn_tok, tt, dst_ap, extra_dst_deps, hn=hn, wcol=0):
                    w3 = wt[:, 0:2048].rearrange("p (c n) -> p c n", c=8)
                    ps, pfree = pp.alloc()
                    PE.waits(pfree + [wtok, hn_tok])
                    for c in range(8):
                        ins = PE.e.matmul(ps[:, 0:512], w3[:, c, wcol * 128:(wcol + 1) * 128], hn[:, c, :], start=(c == 0), stop=(c == 7))
                    tm0 = PE.done(ins)
                    qb, qfree = qbr.alloc()
                    tq = op(ACT, [tm0] + qfree, lambda e: e.activation(out=qb[:], in_=ps[:, 0:512], func=AF.Copy))
                    PE.waits([tq, perm_ready])
                    tm = PE.done(PE.e.matmul(ps[:, 512:1024], permT[:], qb[:], start=True, stop=True))
                    qbr.release(qb, [tm])
                    (ra, rb), rfree = rtr.alloc()
                    t1 = op(DVE, [tm0, tq, tt] + rfree, lambda e: e.tensor_tensor(out=ra[:], in0=ps[:, 0:512], in1=ctab[:], op=ALU.mult))
                    t2 = op(DVE, [tm, tt] + rfree, lambda e: e.tensor_tensor(out=rb[:], in0=ps[:, 512:1024], in1=stab[:], op=ALU.mult))
                    pp.release(ps, [t1, t2])
                    t3 = op(POOL, [t1, t2] + extra_dst_deps, lambda e: e.tensor_tensor(out=dst_ap, in0=ra[:], in1=rb[:], op=ALU.add))
                    rtr.release((ra, rb), [t3])
                    return tm0, t3

                for half in halves:
                    kt_tok = None
                    v_tok = None
                    hnfree = [list(A["hn_free"]), list(A["ot_free"][0]) + list(A["ot_free"][1])]
                    thd = {0: load_h(0, src)}
                    tt = load_tabs(0)
                    sqt = norm_squares(hbs[0], 512, sq8, [thd[0]] + A["sq_free"], True)
                    tn, tmm = norm_finish(hbs[0], 512, 0, spl, hnb[0], sq8, sqt, [thd[0]], hnfree[0])
                    A["sq_free"] = [tmm]
                    A["hb_free"][0] = [tn]
                    if NB > 1:
                        thd[1] = load_h(1, src)
                    lastmm = None
                    for b in range(NB):
                        cs = slice(b * 512, (b + 1) * 512)
                        hcur = hnb[b % 2]
                        if b + 1 < NB:
                            sqt_n = norm_squares(hbs[(b + 1) % 2], 512, sq8, [thd[b + 1]] + A["sq_free"], True)
                        for kp in range(NKC // 2):
                            wi, wt, wtok = feeder.next()
                            for j_ in range(2):
                                tm, t3 = proj_rope(wt, wtok, tn, tt, KT[:, 2 * kp + j_, cs], A["kt_free"], hn=hcur, wcol=j_)
                                kt_tok = t3
                                lastmm = tm
                            feeder.release(wi, [tm])
                        A["tab_free"] = [DVE.last]
                        if b + 1 < NB:
                            tt_n = load_tabs(b + 1)
                            tn_n, tmm = norm_finish(hbs[(b + 1) % 2], 512, 0, spl, hnb[(b + 1) % 2], sq8, sqt_n, [thd[b + 1]], hnfree[(b + 1) % 2])
                            A["sq_free"] = [tmm]
                            A["hb_free"][(b + 1) % 2] = [tn_n]
                            if b + 2 < NB:
                                thd[b + 2] = load_h(b + 2, src)
                        for vt in range(NVT):
                            wi, wt, wtok = feeder.next()
                            w3 = wt[:, 0:2048].rearrange("p (c n) -> p c n", c=8)
                            ps, pfree = pp.alloc()
                            PE.waits(pfree + [wtok, tn])
                            ins = None
                            for tq in range(4):
                                for c in range(8):
                                    ins = PE.e.matmul(ps[:, tq * 256:(tq + 1) * 256], hcur[:, c, tq * 128:(tq + 1) * 128], w3[:, c, :], start=(c == 0), stop=(c == 7))
                            tm = PE.done(ins)
                            feeder.release(wi, [tm])
                            te = op(ACT, [tm] + A["kt_free"], lambda e: e.activation(out=V[:, b * 4:(b + 1) * 4, vt * 256:(vt + 1) * 256], in_=ps[:, 0:1024].rearrange("p (a n) -> p a n", a=4), func=AF.Copy))
                            pp.release(ps, [te])
                            v_tok = te
                            lastmm = tm
                        hnfree[b % 2] = [lastmm]
                        if b + 1 < NB:
                            tn, tt = tn_n, tt_n
                    A["hn_free"] = [lastmm]
                    A["ot_free"] = [[lastmm], [lastmm]]
                    kv_ready = [kt_tok, v_tok, POOL.last, ACT.last]

                    use_acc = (lt == 0 and half == 1)
                    groups = [(q_,) for q_ in range(NQC)] if lt == 0 else [(2 * q_, 2 * q_ + 1) for q_ in range(NQC // 2)]
                    NG = len(groups)

                    def pro_norm(b, th, sqt):
                        tn, tmm = norm_finish(hbs[b % 2], 512, 0, spl, hn, sq8, sqt, [th], A["hn_free"])
                        A["sq_free"] = [tmm]
                        return tn

                    def pro_q(b, tn, th, tt):
                        cs = slice(b * 512, (b + 1) * 512)
                        qtoks = []
                        lastq = None
                        for qp in range(NQC // 2):
                            wi, wt, wtok = feeder.next()
                            for j_ in range(2):
                                tm, t3 = proj_rope(wt, wtok, tn, tt, QT[:, 2 * qp + j_, :], A["qt_free"], wcol=j_)
                                qtoks.append(t3)
                                lastq = tm
                            feeder.release(wi, [tm])
                        A["tab_free"] = [DVE.last]
                        A["hn_free"] = [lastq]
                        if use_acc:
                            th = dma(SP, [tn], d_h[b % 2], hbs[b % 2][:], hview(hY)[:, :, cs])
                        return th, qtoks

                    def attention_group(b, qcs, qtoks):
                        OT = OTs[b % 2]
                        items = []
                        for qc in qcs:
                            if lt == 0:
                                ks = [(kc, 0, 512, 0) for kc in range(NT)]
                            else:
                                base = b * 4
                                rng = {-1: (0, 128), 0: (0, 256), 1: (0, 512), 3: (256, 512), 4: (384, 512), 2: (0, 512)}
                                ks = [(base + r, rng[r][0], rng[r][1], r + 1) for r in (1, -1, 0, 3, 4, 2) if 0 <= base + r < NT]
                            for i_, k_ in enumerate(ks):
                                items.append((qc, k_, i_ == 0, i_ == len(ks) - 1))
                        accd = {}
                        last_sc = [None]

                        def emit_av(pd):
                            qc, (kc, qa, qb, r), first, last, P, tokp, tacc = pd
                            ac = accd[qc]
                            O_db = ac["O"]
                            PE.waits(tokp + (ac["free"] if first else []) + kv_ready)
                            PE.wait(const_ready)
                            if lt == 0:
                                R_db = ac["R"]
                                vl = V[:, kc, qc * 128:(qc + 1) * 128]
                                PE.e.matmul(O_db[:, 0:512], vl, P[:, 0:512], start=first, stop=last)
                                PE.e.matmul(O_db[:, 512:1024], vl, P[:, 512:1024], start=first, stop=last)
                                ins = PE.e.matmul(R_db[:, 512:1024], ones[:], P[:, 512:1024], start=first, stop=last)
                            else:
                                gA = 0 if qc < 4 else 2
                                gB = gA + 1
                                PE.e.matmul(O_db[0:64, qa:qb], V[:, kc, gA * 64:(gA + 1) * 64], P[:, qa:qb], start=first, stop=last)
                                PE.e.matmul(O_db[64:128, qa:qb], V[:, kc, gB * 64:(gB + 1) * 64], P[:, 512 + qa:512 + qb], start=first, stop=last)
                                PE.e.matmul(O_db[0:64, 512 + qa:512 + qb], ones[:, 0:64], P[:, qa:qb], start=first, stop=last)
                                ins = PE.e.matmul(O_db[64:128, 512 + qa:512 + qb], ones[:, 0:64], P[:, 512 + qa:512 + qb], start=first, stop=last)
                            tk = PE.done(ins)
                            pr.release(P, [tk] + ([tacc] if tacc is not None else []))
                            if last:
                                epilogue(qc, tk, tacc)

                        def epilogue(qc, av_tok, tacc):
                            ac = accd[qc]
                            O_db = ac["O"]
                            if lt == 0:
                                R_db = ac["R"]
                                od = ods[b % 2]
                                acc1 = ac["acc1"]
                                PE.waits([tacc, av_tok, const_ready])
                                tokR = PE.done(PE.e.matmul(R_db[:, 0:512], onesF[:], acc1[:], start=True, stop=True))
                                acc_state["free"][ac["ai"]] = [tokR]
                                tc_ = op(DVE, [av_tok], lambda e: e.tensor_copy(out=o12[:], in_=O_db[:, :]))
                                pp.release(O_db, [tc_])
                                t1 = op(ACT, [tokR, DVE.last], lambda e: e.activation(out=rec[:], in_=R_db[:, :], func=AF.Ln))
                                pp.release(R_db, [t1])
                                t1 = op(ACT, [t1], lambda e: e.activation(out=rec[:], in_=rec[:], func=AF.Exp, scale=-1.0))
                                t2 = op(DVE, [t1, tc_], lambda e: e.tensor_tensor(out=o12[:], in0=o12[:], in1=rec[:], op=ALU.mult))
                                op(DVE, [t2, lay_ready] + A["od_free"][b % 2], lambda e: e.scalar_tensor_tensor(out=od[:, qc, :], in0=o12[:, 512:1024], scalar=nlam, in1=o12[:, 0:512], op0=ALU.mult, op1=ALU.add))
                            else:
                                t1 = op(DVE, [av_tok, lay_ready], lambda e: e.tensor_scalar(out=rec[:, 0:512], in0=O_db[:, 512:1024], scalar1=es_t[:, qc:qc + 1], scalar2=None, op0=ALU.add))
                                t1 = op(ACT, [t1], lambda e: e.activation(out=rec[:, 0:512], in_=rec[:, 0:512], func=AF.Ln))
                                t1 = op(ACT, [t1], lambda e: e.activation(out=rec[:, 0:512], in_=rec[:, 0:512], func=AF.Exp, scale=-1.0))
                                t2 = op(DVE, [t1] + A["ot_free"][b % 2], lambda e: e.tensor_tensor(out=OT[:, qc, :], in0=O_db[:, 0:512], in1=rec[:, 0:512], op=ALU.mult))
                                pp.release(O_db, [t2])

                        pendq = []
                        DEPTH = 2
                        tacc = None
                        for (qc, k_, first, last) in items:
                            kc, qa, qb, r = k_
                            if first:
                                ac = {}
                                ac["O"], ofree = pp.alloc()
                                ac["free"] = list(ofree)
                                if lt == 0:
                                    ac["R"], rfree = pp.alloc()
                                    ac["free"] += rfree
                                    ac["ai"] = acc_state["n"] % 2
                                    acc_state["n"] += 1
                                    ac["acc1"] = accs[ac["ai"]]
                                accd[qc] = ac
                            kch = qc if lt == 0 else (0 if qc < 4 else 1)
                            S_db, sfree = pp.alloc()
                            PE.waits(sfree + [qtoks[qc]] + kv_ready)
                            ks = slice(kc * 128, (kc + 1) * 128)
                            if lt == 0:
                                PE.e.matmul(S_db[:, qa:qb], KT[0:64, kch, ks], QT[0:64, qc, qa:qb], start=True, stop=True)
                                ts_ = PE.done(PE.e.matmul(S_db[:, 512 + qa:512 + qb], KT[64:128, kch, ks], QT[64:128, qc, qa:qb], start=True, stop=True))
                            else:
                                PE.wait(tmk)
                                PE.e.matmul(S_db[:, qa:qb], KT[0:64, kch, ks], QT[0:64, qc, qa:qb], start=True, stop=False)
                                PE.e.matmul(S_db[:, 512 + qa:512 + qb], KT[64:128, kch, ks], QT[64:128, qc, qa:qb], start=True, stop=False)
                                PE.e.matmul(S_db[:, qa:qb], ident, mask[:, r, qa:qb], start=False, stop=True)
                                ts_ = PE.done(PE.e.matmul(S_db[:, 512 + qa:512 + qb], ident, mask[:, r, qa:qb], start=False, stop=True))
                            last_sc[0] = ts_
                            if len(pendq) >= DEPTH:
                                emit_av(pendq.pop(0))
                            P, pfree = pr.alloc()
                            if qa == 0 and qb == 512:
                                te = op(ACT, [ts_] + pfree, lambda e: e.activation(out=P[:, :], in_=S_db[:, :], func=AF.Exp, scale=0.125))
                            else:
                                te = op(ACT, [ts_] + pfree, lambda e: e.activation(out=P[:, :].rearrange("p (a n) -> p a n", a=2)[:, :, qa:qb], in_=S_db[:, :].rearrange("p (a n) -> p a n", a=2)[:, :, qa:qb], func=AF.Exp, scale=0.125))
                            pp.release(S_db, [te])
                            if lt == 1:
                                pendq.append((qc, k_, first, last, P, [te], None))
                            else:
                                acc1 = accd[qc]["acc1"]
                                if first:
                                    tacc = op(DVE, [te] + acc_state["free"][accd[qc]["ai"]], lambda e: e.tensor_copy(out=acc1[:], in_=P[:, 0:512]))
                                else:
                                    tacc = op(DVE, [te, tacc], lambda e: e.tensor_tensor(out=acc1[:], in0=acc1[:], in1=P[:, 0:512], op=ALU.add))
                                pendq.append((qc, k_, first, last, P, [te], tacc))
                        while pendq:
                            emit_av(pendq.pop(0))
                        return last_sc[0]

                    def subln_squares(b):
                        od = ods[b % 2]
                        toks = []
                        for qc in range(NQC):
                            toks.append(op(POOL, [DVE.last] + A["sq_free"], lambda e: e.tensor_tensor(out=sq8[:, qc, :], in0=od[:, qc, :], in1=od[:, qc, :], op=ALU.mult)))
                        return toks

                    def epi_subln(b, sqtoks):
                        od = ods[b % 2]
                        OT = OTs[b % 2]
                        tm = None
                        for pair in range(NQC // 2):
                            ps, pfree = pp.alloc()
                            PE.waits(pfree + sqtoks + [const_ready])
                            PE.e.matmul(ps[:, 0:512], onesH[:], sq8[:, 2 * pair, :], start=True, stop=True)
                            tm = PE.done(PE.e.matmul(ps[:, 512:1024], onesH[:], sq8[:, 2 * pair + 1, :], start=True, stop=True))
                            t1 = op(ACT, [tm, DVE.last], lambda e: e.activation(out=rec[:], in_=ps[:, :], func=AF.Ln, bias=eps_ap, scale=1.0))
                            pp.release(ps, [t1])
                            t2 = op(ACT, [t1], lambda e: e.activation(out=rec[:], in_=rec[:], func=AF.Exp, scale=-0.5))
                            for j_ in range(2):
                                qc = 2 * pair + j_
                                op(DVE, [t2, lay_ready] + A["ot_free"][b % 2], lambda e: e.scalar_tensor_tensor(out=OT[:, qc, :], in0=od[:, qc, :], scalar=sublnS, in1=rec[:, j_ * 512:(j_ + 1) * 512], op0=ALU.mult, op1=ALU.mult))
                        A["sq_free"] = [tm]
                        A["od_free"][b % 2] = [DVE.last, POOL.last]

                    def epi_wo(b, th_acc, part, nparts):
                        OT = OTs[b % 2]
                        hb = hbs[b % 2]
                        cs = slice(b * 512, (b + 1) * 512)
                        ot_ready = DVE.last
                        ntile = 8 if lt == 0 else 4
                        per = ntile // nparts
                        last_mm = None
                        for ti in range(part * per, (part + 1) * per):
                            wi, wt, wtok = feeder.next()
                            if lt == 0:
                                m = ti
                                w3 = wt[:, 0:512].rearrange("p (c n) -> p c n", c=4)
                                ps, pfree = pp.alloc()
                                PE.waits(pfree + [wtok, ot_ready])
                                for c in range(4):
                                    ins = PE.e.matmul(ps[:, 0:512], w3[:, c, :], OT[:, c, :], start=(c == 0), stop=(c == 3))
                                tm = PE.done(ins)
                                feeder.release(wi, [tm])
                                ta = op(DVE, [tm, th_acc], lambda e: e.tensor_tensor(out=hb[:, m, :], in0=ps[:, 0:512], in1=hb[:, m, :], op=ALU.add))
                                pp.release(ps, [ta])
                            else:
                                mp = ti
                                w3 = wt[:, 0:2048].rearrange("p (c n) -> p c n", c=8)
                                ps, pfree = pp.alloc()
                                PE.waits(pfree + [wtok, ot_ready])
                                for mm in range(2):
                                    for c in range(8):
                                        ins = PE.e.matmul(ps[:, mm * 512:(mm + 1) * 512], w3[:, c, mm * 128:(mm + 1) * 128], OT[:, c, :], start=(c == 0), stop=(c == 7))
                                tm = PE.done(ins)
                                feeder.release(wi, [tm])
                                for mm in range(2):
                                    m = 2 * mp + mm
                                    ta = op(DVE, [tm, th_acc], lambda e: e.tensor_tensor(out=hb[:, m, :], in0=ps[:, mm * 512:(mm + 1) * 512], in1=hb[:, m, :], op=ALU.add))
                                pp.release(ps, [ta])
                            last_mm = tm
                        if part == nparts - 1:
                            A["ot_free"][b % 2] = [last_mm]
                            tst = dma(SP, [DVE.last], d_st[b % 2], hview(hY)[:, :, cs], hb[:])
                            A["hb_free"][b % 2] = [tst]

                    def epi_list(b, th_acc):
                        parts = []
                        if lt == 0:
                            sqtoks = subln_squares(b)
                            parts.append(lambda: epi_subln(b, sqtoks))
                            parts.append(lambda: epi_wo(b, th_acc, 0, 1))
                        else:
                            parts.append(lambda: epi_wo(b, th_acc, 0, 2))
                            parts.append(lambda: epi_wo(b, th_acc, 1, 2))
                        return parts

                    A["hb_free"] = [[DVE.last, POOL.last], [DVE.last, POOL.last]]
                    if half == halves[0] and li + 1 < L:
                        queue_casts(li + 1)
                    th = load_h(0, src)
                    tt = load_tabs(0)
                    sqt = norm_squares(hbs[0], 512, sq8, [th] + A["sq_free"], True)
                    tn = pro_norm(0, th, sqt)
                    cur = pro_q(0, tn, th, tt)
                    pending = []
                    for b in range(NB):
                        th_acc, qtoks = cur
                        nxt = {}
                        sched = [[] for _ in range(NG)]
                        epi = list(pending)
                        if lt == 0:
                            for i_, f_ in enumerate(epi):
                                sched[i_] += [f_]
                            k = 1 if epi else 0
                        else:
                            for i_, f_ in enumerate(epi):
                                sched[i_] += [f_]
                            k = max(0, len(epi) - 1)
                        if b + 1 < NB:
                            def _loads(bb=b + 1):
                                nxt["th"] = load_h(bb, src)
                                nxt["tt"] = load_tabs(bb)
                            def _squares(bb=b + 1):
                                nxt["sqt"] = norm_squares(hbs[bb % 2], 512, sq8, [nxt["th"]] + A["sq_free"], False)
                            def _norm(bb=b + 1):
                                nxt["tn"] = pro_norm(bb, nxt["th"], nxt["sqt"])
                            sched[k] += [_loads, _squares]
                            sched[k + 1] += [_norm]
                        att_last = None
                        for gi_, grp in enumerate(groups):
                            att_last = attention_group(b, grp, qtoks)
                            pop_cast()
                            for f_ in sched[gi_]:
                                f_()
                        A["qt_free"] = [att_last]
                        pending = epi_list(b, th_acc)
                        if b + 1 < NB:
                            cur = pro_q(b + 1, nxt["tn"], nxt["th"], nxt["tt"])
                    for f in pending:
                        f()
                    while cast_queue:
                        pop_cast()
                    barrier()
                    A = {"hb_free": [[], []], "hn_free": [], "qt_free": [], "ot_free": [[], []], "od_free": [[], []],
                         "tab_free": [], "sq_free": [], "kt_free": []}
                barrier()

            with ExitStack() as fsc:
                NW = n_win
                hwS = [[sb(f"hw{i}", [128, 8, 512], F32, fsc) for i in range(NW)] for _ in range(2)]
                pwS = [[sb(f"pw{i}", [128, 2, 512], BF16, fsc) for i in range(NW)] for _ in range(2)]
                hnw = [sb(f"hnw{i}", [128, 8, 512], BF16, fsc) for i in range(NW)]
                aT = [sb(f"aT{i}", [128, NFC, 512], BF16, fsc) for i in range(NW)]
                sq8f = sb("sq8f", [128, 8, 512], BF16, fsc)
                cv_items = [(sb(f"tg{i}", [128, 512], F32, fsc), sb(f"tv{i}", [128, 512], F32, fsc), sb(f"sg{i}", [128, 512], F32, fsc)) for i in range(3)]
                cvr = Fifo(cv_items)
                pl_items = [(sb(f"sgm{i}", [128, 512], F32, fsc), sb(f"ptmp{i}", [128, 512], F32, fsc)) for i in range(2)]
                plr = Fifo(pl_items)
                dst = dr["outT"] if is_last else hX
                wl = window_list(S)
                passes = [wl[p0:p0 + NW] for p0 in range(0, len(wl), NW)]
                hw_free = [[[] for _ in range(NW)] for _ in range(2)]
                pw_free = [[[] for _ in range(NW)] for _ in range(2)]
                hn_free = [[] for _ in range(NW)]
                a_free = [[] for _ in range(NW)]
                B = {"sq_free": []}
                cw = lambda t, fc: spl[:, 32 + 44 * t + fc: 32 + 44 * t + fc + 1]
                cbv = lambda fc: spl[:, 164 + fc: 165 + fc]

                def issue_loads(pi):
                    st_ = pi % 2
                    h_ready = []
                    p_ready = []
                    for wi_, (s_, e_) in enumerate(passes[pi]):
                        W = e_ - s_
                        hwt = hwS[st_][wi_]
                        lo = s_ - 1
                        hi = e_ + 1
                        deps = list(hw_free[st_][wi_])
                        toks = []
                        c0 = 0
                        if lo < 0:
                            toks.append(op(DVE, deps, lambda e: e.memset(hwt[:, :, 0:1], 0.0)))
                            lo = 0
                            c0 = 1
                        if hi > S:
                            toks.append(op(DVE, deps, lambda e: e.memset(hwt[:, :, W + 1:W + 2], 0.0)))
                            hi = S
                        toks.append(dma(SP, deps, d_h[wi_], hwt[:, :, c0:c0 + (hi - lo)], hview(hY)[:, :, lo:hi]))
                        p_ready.append(dma(POOL, pw_free[st_][wi_], d_p[wi_], pwS[st_][wi_][:, :, 0:W], dr["pT"][li].rearrange("(c p) t -> p c t", p=128)[:, :, s_:e_]))
                        h_ready.append(toks)
                    return h_ready, p_ready

                loads = issue_loads(0)
                for pi, wins in enumerate(passes):
                    st_ = pi % 2
                    hw = hwS[st_]
                    pw = pwS[st_]
                    nw = len(wins)
                    Wd = [e_ - s_ for (s_, e_) in wins]
                    h_ready, p_ready = loads
                    hn_ready = []
                    for wi_ in range(nw):
                        sqt = norm_squares(hw[wi_], Wd[wi_] + 2, sq8f, h_ready[wi_] + B["sq_free"], True)
                        tn, tmm = norm_finish(hw[wi_], Wd[wi_] + 2, 8, spl, hnw[wi_], sq8f, sqt, h_ready[wi_], hn_free[wi_])
                        B["sq_free"] = [tmm]
                        hn_ready.append(tn)
                    a_tok = [None] * nw
                    for jf in range(NFC):
                        fi, wt, wtok = feeder.next()
                        w3 = wt[:, 0:2048].rearrange("p (c n) -> p c n", c=8)
                        tm = None
                        for wi_ in range(nw):
                            W = Wd[wi_]
                            ps, pfree = pp.alloc()
                            PE.waits(pfree + [wtok, hn_ready[wi_]])
                            for c in range(8):
                                PE.e.matmul(ps[:, 0:W + 2], w3[:, c, 0:128], hnw[wi_][:, c, 0:W + 2], start=(c == 0), stop=(c == 7))
                            for c in range(8):
                                ins = PE.e.matmul(ps[:, 512:512 + W + 2], w3[:, c, 128:256], hnw[wi_][:, c, 0:W + 2], start=(c == 0), stop=(c == 7))
                            tm = PE.done(ins)
                            (tg, tv, sg), cfree = cvr.alloc()
                            fg, fv = jf, NFC + jf
                            g1 = op(ACT, [tm] + cfree, lambda e: e.activation(out=tg[:, 0:W], in_=ps[:, 0:W], func=AF.Identity, scale=cw(0, fg), bias=cbv(fg)))
                            v1 = op(ACT, [tm] + cfree, lambda e: e.activation(out=tv[:, 0:W], in_=ps[:, 512:512 + W], func=AF.Identity, scale=cw(0, fv), bias=cbv(fv)))
                            g2 = op(DVE, [g1], lambda e: e.scalar_tensor_tensor(out=tg[:, 0:W], in0=ps[:, 1:W + 1], scalar=cw(1, fg), in1=tg[:, 0:W], op0=ALU.mult, op1=ALU.add))
                            v2 = op(DVE, [v1], lambda e: e.scalar_tensor_tensor(out=tv[:, 0:W], in0=ps[:, 513:513 + W], scalar=cw(1, fv), in1=tv[:, 0:W], op0=ALU.mult, op1=ALU.add))
                            g3 = op(DVE, [g2], lambda e: e.scalar_tensor_tensor(out=tg[:, 0:W], in0=ps[:, 2:W + 2], scalar=cw(2, fg), in1=tg[:, 0:W], op0=ALU.mult, op1=ALU.add))
                            v3 = op(DVE, [v2], lambda e: e.scalar_tensor_tensor(out=tv[:, 0:W], in0=ps[:, 514:514 + W], scalar=cw(2, fv), in1=tv[:, 0:W], op0=ALU.mult, op1=ALU.add))
                            pp.release(ps, [g3, v3])
                            s1 = op(ACT, [g3], lambda e: e.activation(out=sg[:, 0:W], in_=tg[:, 0:W], func=AF.Silu))
                            a_tok[wi_] = op(POOL, [s1, v3] + a_free[wi_], lambda e: e.tensor_tensor(out=aT[wi_][:, jf, 0:W], in0=sg[:, 0:W], in1=tv[:, 0:W], op=ALU.mult))
                            cvr.release((tg, tv, sg), [a_tok[wi_]])
                        feeder.release(fi, [tm])
                    loads_next = issue_loads(pi + 1) if pi + 1 < len(passes) else None
                    h2_tok = [None] * nw
                    for m in range(8):
                        dbs = [None] * nw
                        for half in range(2):
                            fi, wt, wtok = feeder.next()
                            w3 = wt[:, 0:1408].rearrange("p (c n) -> p c n", c=11)
                            tm = None
                            for wi_ in range(nw):
                                W = Wd[wi_]
                                if half == 0:
                                    ps, pfree = pp.alloc()
                                    dbs[wi_] = ps
                                    PE.waits(pfree)
                                ps = dbs[wi_]
                                PE.waits([wtok, a_tok[wi_]])
                                for kk in range(11):
                                    ins = PE.e.matmul(ps[:, 0:W], w3[:, kk, :], aT[wi_][:, half * 11 + kk, 0:W], start=(half == 0 and kk == 0), stop=(half == 1 and kk == 10))
                                tm = PE.done(ins)
                                if half == 1:
                                    h2_tok[wi_] = op(DVE, [tm] + h_ready[wi_] + [hn_ready[wi_]], lambda e: e.tensor_tensor(out=hw[wi_][:, m, 1:W + 1], in0=ps[:, 0:W], in1=hw[wi_][:, m, 1:W + 1], op=ALU.add))
                                    pp.release(ps, [h2_tok[wi_]])
                                    a_free[wi_] = [tm]
                            feeder.release(fi, [tm])
                    hn2_ready = []
                    for wi_ in range(nw):
                        sqt = norm_squares(hw[wi_], Wd[wi_], sq8f, [h2_tok[wi_]] + B["sq_free"], True, off=1)
                        tn, tmm = norm_finish(hw[wi_], Wd[wi_], 16, spl, hnw[wi_], sq8f, sqt, [h2_tok[wi_]], [PE.last], off=1)
                        B["sq_free"] = [tmm]
                        hn2_ready.append(tn)
                    gate_tiles = [feeder.next() for _ in range(4)]
                    pj_i, pj_t, pj_tok = feeder.next()
                    pj3 = pj_t[:, 0:2048].rearrange("p (c n) -> p c n", c=2)
                    h3_tok = [None] * nw
                    tm = None
                    for mp in range(4):
                        fi, wt, wtok = gate_tiles[mp]
                        w3 = wt[:, 0:2048].rearrange("p (c n) -> p c n", c=8)
                        for mm in range(2):
                            m = 2 * mp + mm
                            for wi_ in range(nw):
                                W = Wd[wi_]
                                ps, pfree = pp.alloc()
                                PE.waits(pfree + [wtok, pj_tok, hn2_ready[wi_], p_ready[wi_]])
                                for c in range(8):
                                    PE.e.matmul(ps[:, 0:W], w3[:, c, mm * 128:(mm + 1) * 128], hnw[wi_][:, c, 0:W], start=(c == 0), stop=(c == 7))
                                for c in range(2):
                                    ins = PE.e.matmul(ps[:, 512:512 + W], pj3[:, c, m * 128:(m + 1) * 128], pw[wi_][:, c, 0:W], start=(c == 0), stop=(c == 1))
                                tm = PE.done(ins)
                                (sgm, ptmp), lfree = plr.alloc()
                                s1 = op(ACT, [tm] + lfree, lambda e: e.activation(out=sgm[:, 0:W], in_=ps[:, 0:W], func=AF.Sigmoid))
                                s2 = op(DVE, [s1], lambda e: e.tensor_tensor(out=ptmp[:, 0:W], in0=ps[:, 512:512 + W], in1=sgm[:, 0:W], op=ALU.mult))
                                pp.release(ps, [s2])
                                h3_tok[wi_] = op(POOL, [s2, hn2_ready[wi_]], lambda e: e.tensor_tensor(out=hw[wi_][:, m, 1:W + 1], in0=hw[wi_][:, m, 1:W + 1], in1=ptmp[:, 0:W], op=ALU.add))
                                plr.release((sgm, ptmp), [h3_tok[wi_]])
                        feeder.release(fi, [tm])
                    feeder.release(pj_i, [tm])
                    for wi_, (s_, e_) in enumerate(wins):
                        W = Wd[wi_]
                        tk = h3_tok[wi_]
                        if is_last and apply_final:
                            tk = norm_block(hw[wi_], W, 24, spl, None, [tk], [], off=1, out_f32=hw[wi_])
                        tst = dma(SP, [tk, POOL.last], d_st[wi_], hview(dst)[:, :, s_:e_], hw[wi_][:, :, 1:W + 1])
                        hw_free[st_][wi_] = [tst]
                        pw_free[st_][wi_] = [PE.last]
                        hn_free[wi_] = [PE.last]
                    loads = loads_next
                barrier()
        SP.wait(d_st[0].tok())
        SP.wait(d_st[1].tok())
    return nc


LAYER_TYPES = [0, 1, 0, 1]
_PROGRAM_CACHE = {}


def _get_program(S, layers, apply_final, n_win):
    key = (S, tuple(layers), apply_final, n_win)
    if key not in _PROGRAM_CACHE:
        _PROGRAM_CACHE[key] = build_program(S, list(layers), apply_final, n_win)
    return _PROGRAM_CACHE[key]


def run_layers(xT_list, inp, layer_ids, pos_list, apply_final, n_win=2, core_ids=None):
    S = xT_list[0].shape[1]
    layers = [(gi, LAYER_TYPES[gi]) for gi in layer_ids]
    nc = _get_program(S, layers, apply_final, n_win)
    cst, msk = host_consts()
    packs = [pack_layer(LAYER_TYPES[gi], gi // 2, inp, gi) for gi in layer_ids]
    in_maps = []
    ncore = len(xT_list)
    for ci in range(ncore):
        m = {"xT": np.ascontiguousarray(xT_list[ci], dtype=np.float32), "pos": pos_list[ci], "cst": cst, "msk": msk, "perm": host_perm()}
        m["pT"] = np.ascontiguousarray(np.stack([inp["p"][gi][ci].T for gi in layer_ids]), dtype=np.float32)
        for li, pk in enumerate(packs):
            m[f"wkv{li}"] = pk["wkv"]
            m[f"wa{li}"] = pk["wa"]
            m[f"wb{li}"] = pk["wb"]
            m[f"sp{li}"] = pk["sp"]
        in_maps.append(m)
    res = run_bass_kernel_spmd(nc, in_maps, core_ids=list(range(ncore)) if core_ids is None else core_ids)
    return [r["outT"] for r in res.results]


FUSED = True


def kernel(**inputs):
    inp = {k: np.asarray(v) for k, v in inputs.items()}
    x = inp["x"]
    B, S, _ = x.shape
    xT = [np.ascontiguousarray(x[b].T) for b in range(B)]
    pos = [np.ascontiguousarray(inp["positions"][b].reshape(1, S).astype(np.int32)) for b in range(B)]
    if FUSED:
        outs = run_layers(xT, inp, [0, 1, 2, 3], pos, True)
    else:
        cur = xT
        for gi in range(4):
            cur = run_layers(cur, inp, [gi], pos, gi == 3)
        outs = cur
    return np.stack([o.T for o in outs]).astype(np.float32)
```

```python
import math
from contextlib import ExitStack

import numpy as np
import concourse.bass as bass
import concourse.mybir as mybir
from concourse.bass_utils import run_bass_kernel_spmd

F32 = mybir.dt.float32
BF16 = mybir.dt.bfloat16
I32 = mybir.dt.int32
AF = mybir.ActivationFunctionType
ALU = mybir.AluOpType
AX = mybir.AxisListType

D = 1024
DFF = 2816
NFC = 22
PLE = 256
EPS = 1e-6
TE = 2048
NSP = 480
NSLOT = 5
TWO_PI = 2.0 * math.pi
C1 = 6.28125
C2 = TWO_PI - C1


def _tile_kc(W, cols, nk=None):
    K = W.shape[0]
    kc = K // 128
    t = W[:, cols].reshape(kc, 128, len(cols)).transpose(1, 0, 2).reshape(128, -1)
    out = np.zeros((128, TE), np.float32)
    out[:, : t.shape[1]] = t
    return out


def _rot_cols(cols):
    cols = np.asarray(cols)
    r = cols.reshape(-1, 2, 32)[:, ::-1, :].reshape(-1)
    return r


def swa_chunk_heads(c):
    return (c, 4 + c) if c < 4 else (8 + c - 4, 12 + c - 4)


def pack_layer(lt, j, inp, i):
    out = {}
    ar = np.arange
    if lt == 0:
        wqkv = inp["diff_w_qkv"][j]
        wo = inp["diff_w_o"][j]
        kv = []
        for half in range(2):
            for c in range(4 * half, 4 * half + 4, 2):
                kv.append(_tile_kc(wqkv, 1024 + c * 128 + ar(256)))
            for vt in range(2 * half, 2 * half + 2):
                kv.append(_tile_kc(wqkv, 2048 + vt * 256 + ar(256)))
        a = []
        for half in range(2):
            for c in range(4 * half, 4 * half + 4, 2):
                a.append(_tile_kc(wqkv, c * 128 + ar(256)))
            wo_h = wo[half * 512:(half + 1) * 512]
            for mq in range(2):
                a.append(_tile_kc(wo_h, mq * 512 + ar(512)))
    else:
        wqkv = inp["swa_w_qkv"][j]
        wo = inp["swa_w_o"][j]
        kv = []
        kv.append(_tile_kc(wqkv, 1024 + ar(256)))
        kv.append(_tile_kc(wqkv, 1280 + ar(256)))
        a = []
        rowperm = []
        qcols = []
        for c in range(8):
            hA, hB = swa_chunk_heads(c)
            cols = np.concatenate([hA * 64 + ar(64), hB * 64 + ar(64)])
            rowperm.append(cols)
            qcols.append(cols)
        for c in range(0, 8, 2):
            a.append(_tile_kc(wqkv, np.concatenate([qcols[c], qcols[c + 1]])))
        wo_p = wo[np.concatenate(rowperm)]
        for mp in range(4):
            a.append(_tile_kc(wo_p, mp * 256 + ar(256)))
    out["wkv"] = np.stack(kv)
    out["wa"] = np.stack(a)
    b = []
    wup = inp["ffn_w_up"][i]
    for jf in range(NFC):
        b.append(_tile_kc(wup, np.concatenate([jf * 128 + ar(128), DFF + jf * 128 + ar(128)])))
    wdn = inp["ffn_w_down"][i]
    for m in range(8):
        for half in range(2):
            b.append(_tile_kc(wdn[half * 11 * 128:(half + 1) * 11 * 128], m * 128 + ar(128)))
    wg = inp["ple_w_gate"][i]
    for mp in range(4):
        b.append(_tile_kc(wg, mp * 256 + ar(256)))
    b.append(_tile_kc(inp["ple_w_proj"][i], ar(1024)))
    out["wb"] = np.stack(b)
    sp = np.zeros((128, NSP), np.float32)
    sp[:, 0:8] = inp["attn_norm"][i].reshape(8, 128).T
    sp[:, 8:16] = inp["ffn_norm"][i].reshape(8, 128).T
    sp[:, 16:24] = inp["ple_norm"][i].reshape(8, 128).T
    sp[:, 24:32] = inp["final_norm"].reshape(8, 128).T
    cw = inp["ffn_conv_w"][i]
    for t in range(3):
        sp[:, 32 + 44 * t: 32 + 44 * (t + 1)] = cw[t].reshape(44, 128).T
    sp[:, 164:208] = inp["ffn_conv_b"][i].reshape(44, 128).T
    if lt == 0:
        sp[:, 208] = inp["diff_subln"][j]
        sp[:, 217:473] = inp["diff_lambda"][j].reshape(1, 256)
    else:
        sk = inp["swa_sinks"][j]
        for c in range(8):
            hA, hB = swa_chunk_heads(c)
            sp[0:64, 209 + c] = sk[hA]
            sp[64:128, 209 + c] = sk[hB]
    out["sp"] = sp
    return out


def host_consts():
    cst = np.zeros((128, 8), np.float32)
    inv_freq = (10000.0 ** (-np.arange(0, 64, 2, dtype=np.float32) / np.float32(64))).astype(np.float32)
    p = np.arange(128)
    cst[:, 0] = inv_freq[p % 32]
    cst[:, 1] = np.where((p % 64) < 32, -1.0, 1.0)
    cst[:, 2] = EPS
    kk = np.arange(128)[:, None]
    qq = np.arange(512)[None, :]
    masks = np.zeros((128, 6, 512), np.float32)
    for r in range(6):
        rel = r - 1
        masks[:, r, :] = np.where(np.abs(qq - kk - 128 * rel) <= 128, 0.0, -30000.0).astype(np.float32)
    ident = np.eye(128, dtype=np.float32)
    return cst, np.concatenate([masks.reshape(128, 6 * 512), ident], axis=1)


def host_perm():
    p = np.arange(128)
    partner = np.where((p % 64) < 32, p + 32, p - 32)
    pm = np.zeros((128, 128), np.float32)
    pm[partner, p] = 1.0
    return pm


class Eng:
    def __init__(self, nc, es, eng, name, own_sem=True):
        self.e = eng
        self.name = name
        self.sem = es.enter_context(nc.semaphore("s_" + name)) if own_sem else None
        self.cnt = 0
        self.waited = {}
        self.last = None

    def wait(self, tok):
        if tok is None:
            return
        key, sem, val = tok
        if self.waited.get(key, 0) >= val:
            return
        self.e.wait_ge(sem, val)
        self.waited[key] = val

    def waits(self, toks):
        for t in toks:
            self.wait(t)

    def done(self, ins):
        ins.then_inc(self.sem, 1)
        self.cnt += 1
        self.last = (self.name, self.sem, self.cnt)
        return self.last


class DSem:
    def __init__(self, nc, es, name, registry):
        self.sem = es.enter_context(nc.semaphore("d_" + name))
        self.name = "d_" + name
        self.cnt = 0
        registry.append(self)

    def tok(self):
        return (self.name, self.sem, self.cnt)

    def fire(self, ins):
        ins.then_inc(self.sem, 16)
        self.cnt += 16
        return self.tok()


class Fifo:
    def __init__(self, items):
        self.free = [(it, []) for it in items]

    def alloc(self):
        assert self.free, "pool exhausted"
        return self.free.pop(0)

    def release(self, item, toks):
        self.free.append((item, list(toks)))


def window_list(S):
    n = -(-S // 510)
    base = S // n
    rem = S - base * n
    ws = []
    s = 0
    for i in range(n):
        w = base + (1 if i < rem else 0)
        ws.append((s, s + w))
        s += w
    assert s == S
    return ws


def build_program(S, layers, apply_final, n_win=2):
    nc = bass.Bass("TRN2", target_bir_lowering=False)
    L = len(layers)
    NT = S // 128
    NB = S // 512
    dr = {}
    dr["xT"] = nc.dram_tensor("xT", [D, S], F32, kind="ExternalInput").ap()
    dr["pT"] = nc.dram_tensor("pT", [L, PLE, S], F32, kind="ExternalInput").ap()
    dr["pos"] = nc.dram_tensor("pos", [1, S], I32, kind="ExternalInput").ap()
    dr["cst"] = nc.dram_tensor("cst", [128, 8], F32, kind="ExternalInput").ap()
    dr["msk"] = nc.dram_tensor("msk", [128, 6 * 512 + 128], F32, kind="ExternalInput").ap()
    dr["perm"] = nc.dram_tensor("perm", [128, 128], F32, kind="ExternalInput").ap()
    dr["outT"] = nc.dram_tensor("outT", [D, S], F32, kind="ExternalOutput").ap()
    hX = nc.dram_tensor("hX", [D, S], F32, kind="Internal").ap()
    hY = nc.dram_tensor("hY", [D, S], F32, kind="Internal").ap()
    tabC = nc.dram_tensor("tabC", [128, S], F32, kind="Internal").ap()
    tabS = nc.dram_tensor("tabS", [128, S], F32, kind="Internal").ap()
    wsrc, wbf = [], []
    for li, (gi, lt) in enumerate(layers):
        nkv = 8 if lt == 0 else 2
        na = 8 if lt == 0 else 8
        d = {}
        e = {}
        for nm, n in (("wkv", nkv), ("wa", na), ("wb", 43)):
            d[nm] = nc.dram_tensor(f"{nm}{li}", [n, 128, TE], F32, kind="ExternalInput").ap()
            e[nm] = nc.dram_tensor(f"{nm}b{li}", [n, 128, TE], BF16, kind="Internal").ap()
        d["sp"] = nc.dram_tensor(f"sp{li}", [128, NSP], F32, kind="ExternalInput").ap()
        wsrc.append(d)
        wbf.append(e)

    es = ExitStack()
    with es:
        dsems = []
        PE = Eng(nc, es, nc.tensor, "pe")
        ACT = Eng(nc, es, nc.scalar, "act")
        DVE = Eng(nc, es, nc.vector, "dve")
        POOL = Eng(nc, es, nc.gpsimd, "pool")
        SP = Eng(nc, es, nc.sync, "sp", own_sem=False)
        ENGS = [PE, ACT, DVE, POOL]

        uid = [0]

        def sb(name, shape, dt, stack=es):
            uid[0] += 1
            return stack.enter_context(nc.sbuf_tensor(f"{name}_{uid[0]}", shape, dt))

        def op(E, deps, fn):
            E.waits(deps)
            return E.done(fn(E.e))

        def dma(Q, deps, ds, out, in_):
            Q.waits(deps)
            return ds.fire(Q.e.dma_start(out=out, in_=in_))

        def barrier():
            toks = [E.last for E in ENGS if E.last is not None] + [d.tok() for d in dsems if d.cnt > 0 and not d.name.startswith('d_cast')]
            for E in ENGS + [SP]:
                E.waits(toks)

        cst = sb("cst", [128, 8], F32)
        spt = [sb(f"spt{li}", [128, NSP], F32) for li in range(L)]
        ones = sb("ones", [128, 128], BF16)
        onesD = sb("onesD", [128, 128], BF16)
        onesH = sb("onesH", [128, 128], BF16)
        onesF = sb("onesF", [128, 128], F32)
        permT = sb("permT", [128, 128], BF16)
        wring = [sb(f"wr{i}", [128, TE], BF16) for i in range(NSLOT)]
        sq_items = [sb(f"sq{i}", [128, 512], BF16) for i in range(3)]
        rs = sb("rs", [128, 512], F32)
        es_t = sb("es_t", [128, 8], F32)
        lam_t = sb("lam_t", [128, 8], F32)
        lamw = sb("lamw", [128, 128], F32)
        psum = [es.enter_context(nc.psum_tensor(f"ps{i}", [128, 1024], F32)) for i in range(4)]
        pp = Fifo(psum)
        sqr = Fifo(sq_items)

        d_misc = DSem(nc, es, "misc", dsems)
        d_miscp = DSem(nc, es, "miscp", dsems)
        d_cast = [[DSem(nc, es, f"cast{li}_{k}", dsems) for k in range(3)] for li in range(L)]
        d_w = [DSem(nc, es, f"w{i}", dsems) for i in range(NSLOT)]
        d_h = [DSem(nc, es, f"h{i}", dsems) for i in range(2)]
        d_tab = DSem(nc, es, "tab", dsems)
        d_st = [DSem(nc, es, f"st{i}", dsems) for i in range(2)]
        d_p = [DSem(nc, es, f"p{i}", dsems) for i in range(2)]

        t_c = dma(SP, [], d_misc, cst[:], dr["cst"])
        perm_ready = dma(POOL, [], d_miscp, permT[:], dr["perm"])
        for li in range(L):
            t_c = dma(SP, [], d_misc, spt[li][:], wsrc[li]["sp"])
        cast_tok = []
        cast_plan = []
        for li in range(L):
            toks = []
            plans = []
            for k, nm in enumerate(("wkv", "wa", "wb")):
                n = wsrc[li][nm].shape[0]
                src_ = wsrc[li][nm].rearrange("n p f -> (n p) f")
                dst_ = wbf[li][nm].rearrange("n p f -> (n p) f")
                i0 = 0
                pl = []
                while i0 < n:
                    i1 = min(n, i0 + 8)
                    pl.append((dst_[i0 * 128:i1 * 128, :], src_[i0 * 128:i1 * 128, :]))
                    i0 = i1
                plans.append(pl)
                toks.append((d_cast[li][k].name, d_cast[li][k].sem, 16 * len(pl)))
            cast_tok.append(toks)
            cast_plan.append(plans)

        def issue_casts(li_):
            for k in range(3):
                for (dd, ss) in cast_plan[li_][k]:
                    dma(POOL, [], d_cast[li_][k], dd, ss)

        cast_queue = []

        def queue_casts(li_):
            for k in range(3):
                for (dd, ss) in cast_plan[li_][k]:
                    cast_queue.append((d_cast[li_][k], dd, ss))

        def pop_cast():
            if cast_queue:
                ds_, dd, ss = cast_queue.pop(0)
                dma(POOL, [], ds_, dd, ss)

        issue_casts(0)
        c_ready = t_c
        t0 = op(DVE, [], lambda e: e.memset(ones[:], 1.0))
        t0 = op(DVE, [], lambda e: e.memset(onesD[:], 1.0 / 1024))
        t0 = op(DVE, [], lambda e: e.memset(onesF[:], 1.0))
        const_ready = op(DVE, [], lambda e: e.memset(onesH[:], 1.0 / 128))

        invf = cst[:, 0:1]
        sgn = cst[:, 1:2]
        eps_ap = cst[:, 2:3]
        with ExitStack() as ts:
            posi = sb("posi", [128, S], I32, ts)
            tA = sb("tA", [128, S], F32, ts)
            tB = sb("tB", [128, S], F32, ts)
            tK = sb("tK", [128, S], F32, ts)
            tR = sb("tR", [128, S], F32, ts)
            tO = [sb(f"tO{i}", [128, S], F32, ts) for i in range(2)]
            ki = sb("ki", [128, S], I32, ts)
            last_tab = None
            tl = dma(SP, [], d_tab, posi[:], dr["pos"][0:1, :].partition_broadcast(128))
            t = op(DVE, [tl, c_ready], lambda e: e.tensor_copy(out=tA[:], in_=posi[:]))
            t = op(DVE, [t], lambda e: e.tensor_scalar(out=tA[:], in0=tA[:], scalar1=invf, scalar2=None, op0=ALU.mult))
            for which in range(2):
                if which == 1:
                    t = op(DVE, [t], lambda e: e.tensor_scalar(out=tA[:], in0=tA[:], scalar1=float(math.pi / 2), scalar2=None, op0=ALU.add))
                t = op(DVE, [t], lambda e: e.tensor_scalar(out=tB[:], in0=tA[:], scalar1=float(1.0 / TWO_PI), scalar2=None, op0=ALU.mult))
                t = op(DVE, [t], lambda e: e.tensor_copy(out=ki[:], in_=tB[:]))
                t = op(DVE, [t], lambda e: e.tensor_copy(out=tK[:], in_=ki[:]))
                t = op(DVE, [t, ACT.last], lambda e: e.scalar_tensor_tensor(out=tR[:], in0=tK[:], scalar=-C1, in1=tA[:], op0=ALU.mult, op1=ALU.add))
                t = op(DVE, [t], lambda e: e.scalar_tensor_tensor(out=tR[:], in0=tK[:], scalar=-C2, in1=tR[:], op0=ALU.mult, op1=ALU.add))
                t = op(DVE, [t], lambda e: e.tensor_scalar(out=tB[:], in0=tR[:], scalar1=float(math.pi), scalar2=float(-TWO_PI), op0=ALU.is_gt, op1=ALU.mult))
                t = op(DVE, [t], lambda e: e.tensor_tensor(out=tR[:], in0=tR[:], in1=tB[:], op=ALU.add))
                t = op(DVE, [t], lambda e: e.tensor_scalar(out=tR[:], in0=tR[:], scalar1=float(math.pi), scalar2=float(-math.pi), op0=ALU.min, op1=ALU.max))
                to = tO[which]
                ta = op(ACT, [t], lambda e: e.activation(out=to[:], in_=tR[:], func=AF.Sin))
                if which == 0:
                    ta = op(DVE, [ta], lambda e: e.tensor_scalar(out=to[:], in0=to[:], scalar1=sgn, scalar2=None, op0=ALU.mult))
                t = ta
                last_tab = dma(SP, [ta], d_tab, (tabS if which == 0 else tabC)[:, :], to[:])
            barrier()
        tab_ready = last_tab

        class Feeder:
            def __init__(self):
                self.seq = []
                self.issued = 0
                self.pos = 0
                self.rel = {}
                self.ready = {}

            def plan(self, tiles):
                self.seq.extend(tiles)

            def _issue(self, i):
                ap, ctok = self.seq[i]
                deps = [ctok]
                if i >= NSLOT:
                    deps += self.rel[i - NSLOT]
                self.ready[i] = dma(SP, deps, d_w[i % NSLOT], wring[i % NSLOT][:], ap)
                self.issued = i + 1

            def start(self):
                for i in range(min(NSLOT, len(self.seq))):
                    self._issue(i)

            def next(self):
                i = self.pos
                self.pos += 1
                return i, wring[i % NSLOT], self.ready[i]

            def release(self, i, toks):
                self.rel[i] = list(toks)
                if i + NSLOT < len(self.seq):
                    self._issue(i + NSLOT)

        feeder = Feeder()

        def hview(ap):
            return ap.rearrange("(c p) t -> p c t", p=128)

        def norm_block(h, W, g0, spl, hn, deps_h, deps_hn, off=0, out_f32=None):
            ps, pfree = pp.alloc()
            PE.waits(pfree)
            tmm = None
            for c in range(8):
                sq, sfree = sqr.alloc()
                tsq = op(POOL, deps_h + sfree, lambda e: e.tensor_tensor(out=sq[:, 0:W], in0=h[:, c, off:off + W], in1=h[:, c, off:off + W], op=ALU.mult))
                PE.wait(tsq)
                PE.wait(const_ready)
                tmm = PE.done(PE.e.matmul(ps[:, 0:W], onesD[:], sq[:, 0:W], start=(c == 0), stop=(c == 7)))
                sqr.release(sq, [tmm])
            t1 = op(ACT, [tmm, DVE.last], lambda e: e.activation(out=rs[:, 0:W], in_=ps[:, 0:W], func=AF.Ln, bias=eps_ap, scale=1.0))
            pp.release(ps, [t1])
            t2 = op(ACT, [t1], lambda e: e.activation(out=rs[:, 0:W], in_=rs[:, 0:W], func=AF.Exp, scale=-0.5))
            t = None
            for c in range(8):
                if out_f32 is None:
                    t = op(DVE, [t2] + deps_h + deps_hn, lambda e: e.scalar_tensor_tensor(out=hn[:, c, 0:W], in0=h[:, c, off:off + W], scalar=spl[:, g0 + c:g0 + c + 1], in1=rs[:, 0:W], op0=ALU.mult, op1=ALU.mult))
                else:
                    t = op(DVE, [t2] + deps_h + deps_hn, lambda e: e.scalar_tensor_tensor(out=out_f32[:, c, off:off + W], in0=h[:, c, off:off + W], scalar=spl[:, g0 + c:g0 + c + 1], in1=rs[:, 0:W], op0=ALU.mult, op1=ALU.mult))
            return t

        def norm_squares(h, W, sq8, deps, use_act, off=0):
            toks = []
            for c in range(8):
                if use_act and c % 2 == 1:
                    toks.append(op(ACT, deps, lambda e: e.activation(out=sq8[:, c, 0:W], in_=h[:, c, off:off + W], func=AF.Square)))
                else:
                    toks.append(op(POOL, deps, lambda e: e.tensor_tensor(out=sq8[:, c, 0:W], in0=h[:, c, off:off + W], in1=h[:, c, off:off + W], op=ALU.mult)))
            return toks

        def norm_finish(h, W, g0, spl, hn, sq8, sq_toks, deps_h, deps_hn, off=0):
            ps, pfree = pp.alloc()
            PE.waits(pfree + sq_toks + [const_ready])
            ins = None
            for c in range(8):
                ins = PE.e.matmul(ps[:, 0:W], onesD[:], sq8[:, c, 0:W], start=(c == 0), stop=(c == 7))
            tmm = PE.done(ins)
            t1 = op(ACT, [tmm, DVE.last], lambda e: e.activation(out=rs[:, 0:W], in_=ps[:, 0:W], func=AF.Ln, bias=eps_ap, scale=1.0))
            pp.release(ps, [t1])
            t2 = op(ACT, [t1], lambda e: e.activation(out=rs[:, 0:W], in_=rs[:, 0:W], func=AF.Exp, scale=-0.5))
            t = None
            for c in range(8):
                t = op(DVE, [t2] + deps_h + deps_hn, lambda e: e.scalar_tensor_tensor(out=hn[:, c, 0:W], in0=h[:, c, off:off + W], scalar=spl[:, g0 + c:g0 + c + 1], in1=rs[:, 0:W], op0=ALU.mult, op1=ALU.mult))
            return t, tmm

        steps = []

        def tiles_of(li, nm, idxs):
            k = {"wkv": 0, "wa": 1, "wb": 2}[nm]
            return [(wbf[li][nm][i], cast_tok[li][k]) for i in idxs]

        for li, (gi, lt) in enumerate(layers):
            if lt == 0:
                for half in range(2):
                    for b in range(NB):
                        feeder.plan(tiles_of(li, "wkv", range(4 * half, 4 * half + 4)))
                    qt_ = list(range(4 * half, 4 * half + 2))
                    wo_ = list(range(4 * half + 2, 4 * half + 4))
                    feeder.plan(tiles_of(li, "wa", qt_))
                    for b in range(NB):
                        if b + 1 < NB:
                            feeder.plan(tiles_of(li, "wa", qt_))
                        feeder.plan(tiles_of(li, "wa", wo_))
            else:
                for b in range(NB):
                    feeder.plan(tiles_of(li, "wkv", range(2)))
                feeder.plan(tiles_of(li, "wa", range(4)))
                for b in range(NB):
                    if b + 1 < NB:
                        feeder.plan(tiles_of(li, "wa", range(4)))
                    feeder.plan(tiles_of(li, "wa", range(4, 8)))
            wl = window_list(S)
            for p0 in range(0, len(wl), n_win):
                feeder.plan(tiles_of(li, "wb", range(43)))
        feeder.start()

        for li, (gi, lt) in enumerate(layers):
            spl = spt[li]
            src = dr["xT"] if li == 0 else hX
            is_last = (li == L - 1)
            lam_init = 0.8 - 0.6 * math.exp(-0.3 * gi)
            halves = [0, 1] if lt == 0 else [0]
            NQC = 4 if lt == 0 else 8
            NKC = 4 if lt == 0 else 2
            VW = 512 if lt == 0 else 256
            NVT = VW // 256

            if lt == 0:
                l = spl[:, 217:473]
                t = op(DVE, [c_ready], lambda e: e.tensor_tensor(out=lamw[:, 0:64], in0=l[:, 0:64], in1=l[:, 64:128], op=ALU.mult))
                t = op(DVE, [c_ready], lambda e: e.tensor_tensor(out=lamw[:, 64:128], in0=l[:, 128:192], in1=l[:, 192:256], op=ALU.mult))
                t = op(DVE, [t], lambda e: e.tensor_reduce(out=lam_t[:, 0:1], in_=lamw[:, 0:64], axis=AX.X, op=ALU.add))
                t = op(DVE, [t], lambda e: e.tensor_reduce(out=lam_t[:, 1:2], in_=lamw[:, 64:128], axis=AX.X, op=ALU.add))
                t = op(ACT, [t], lambda e: e.activation(out=lam_t[:, 2:4], in_=lam_t[:, 0:2], func=AF.Exp))
                t = op(DVE, [t], lambda e: e.scalar_tensor_tensor(out=lam_t[:, 4:5], in0=lam_t[:, 3:4], scalar=float(-lam_init), in1=lam_t[:, 2:3], op0=ALU.add, op1=ALU.subtract))
                t = op(DVE, [t], lambda e: e.tensor_scalar(out=lam_t[:, 5:6], in0=spl[:, 208:209], scalar1=float(1.0 - lam_init), scalar2=None, op0=ALU.mult))
                lay_ready = t
                nlam = lam_t[:, 4:5]
                sublnS = lam_t[:, 5:6]
            else:
                lay_ready = op(ACT, [c_ready], lambda e: e.activation(out=es_t[:, 0:8], in_=spl[:, 209:217], func=AF.Exp))

            with ExitStack() as asc:
                KT = sb("KT", [128, NKC, S], BF16, asc)
                V = sb("V", [128, NT, VW], BF16, asc)
                hbs = [sb("hb", [128, 8, 512], F32, asc) for _ in range(2)]
                hn = sb("hn", [128, 8, 512], BF16, asc)
                sq8 = sb("sq8", [128, 8, 512], BF16, asc)
                QT = sb("QT", [128, NQC, 512], BF16, asc)
                OTall = sb("OT", [128, 2 * NQC, 512], BF16, asc)
                OTs = [OTall[:, 0:NQC, :], OTall[:, NQC:2 * NQC, :]]
                hnb = [hn, OTall[:, 0:8, :]]
                ctab = sb("ctab", [128, 512], F32, asc)
                stab = sb("stab", [128, 512], F32, asc)
                rt_items = [(sb(f"rt{i}a", [128, 512], F32, asc), sb(f"rt{i}b", [128, 512], F32, asc)) for i in range(2)]
                rtr = Fifo(rt_items)
                qbr = Fifo([sb(f"qb{i}", [128, 512], BF16, asc) for i in range(2)])
                p_items = [sb(f"P{i}", [128, 1024], BF16, asc) for i in range(3 if lt == 0 else 4)]
                pr = Fifo(p_items)
                rec = sb("rec", [128, 1024], F32, asc)
                o12 = sb("o12", [128, 1024], F32, asc)
                if lt == 0:
                    ods = [sb("od", [128, NQC, 512], F32, asc) for _ in range(2)]
                    accs = [sb("acc", [128, 512], F32, asc) for _ in range(2)]
                    acc_state = {"n": 0, "free": [[], []]}
                else:
                    maskf = sb("mask", [128, 6 * 512 + 128], BF16, asc)
                    tmk = dma(POOL, [], d_miscp, maskf[:], dr["msk"])
                    mask = maskf[:, 0:3072].rearrange("p (r q) -> p r q", r=6)
                    ident = maskf[:, 3072:3200]

                A = {"hb_free": [[], []], "hn_free": [], "qt_free": [], "ot_free": [[], []], "od_free": [[], []],
                     "tab_free": [], "sq_free": [], "kt_free": []}

                def load_h(b, srcap):
                    cs = slice(b * 512, (b + 1) * 512)
                    return dma(SP, A["hb_free"][b % 2], d_h[b % 2], hbs[b % 2][:], hview(srcap)[:, :, cs])

                def load_tabs(b):
                    cs = slice(b * 512, (b + 1) * 512)
                    dma(SP, A["tab_free"] + [tab_ready], d_tab, ctab[:], tabC[:, cs])
                    return dma(SP, [], d_tab, stab[:], tabS[:, cs])

                def proj_rope(wt, wtok, hn_tok, tt, dst_ap, extra_dst_deps, hn=hn, wcol=0):
                    w3 = wt[:, 0:2048].rearrange("p (c n) -> p c n", c=8)
                    ps, pfree = pp.alloc()
                    PE.waits(pfree + [wtok, hn_tok])
                    for c in range(8):
                        ins = PE.e.matmul(ps[:, 0:512], w3[:, c, wcol * 128:(wcol + 1) * 128], hn[:, c, :], start=(c == 0), stop=(c == 7))
                    tm0 = PE.done(ins)
                    qb, qfree = qbr.alloc()
                    tq = op(ACT, [tm0] + qfree, lambda e: e.activation(out=qb[:], in_=ps[:, 0:512], func=AF.Copy))
                    PE.waits([tq, perm_ready])
                    tm = PE.done(PE.e.matmul(ps[:, 512:1024], permT[:], qb[:], start=True, stop=True))
                    qbr.release(qb, [tm])
                    (ra, rb), rfree = rtr.alloc()
                    t1 = op(DVE, [tm0, tq, tt] + rfree, lambda e: e.tensor_tensor(out=ra[:], in0=ps[:, 0:512], in1=ctab[:], op=ALU.mult))
                    t2 = op(DVE, [tm, tt] + rfree, lambda e: e.tensor_tensor(out=rb[:], in0=ps[:, 512:1024], in1=stab[:], op=ALU.mult))
                    pp.release(ps, [t1, t2])
                    t3 = op(POOL, [t1, t2] + extra_dst_deps, lambda e: e.tensor_tensor(out=dst_ap, in0=ra[:], in1=rb[:], op=ALU.add))
                    rtr.release((ra, rb), [t3])
                    return tm0, t3

                for half in halves:
                    kt_tok = None
                    v_tok = None
                    hnfree = [list(A["hn_free"]), list(A["ot_free"][0]) + list(A["ot_free"][1])]
                    thd = {0: load_h(0, src)}
                    tt = load_tabs(0)
                    sqt = norm_squares(hbs[0], 512, sq8, [thd[0]] + A["sq_free"], True)
                    tn, tmm = norm_finish(hbs[0], 512, 0, spl, hnb[0], sq8, sqt, [thd[0]], hnfree[0])
                    A["sq_free"] = [tmm]
                    A["hb_free"][0] = [tn]
                    if NB > 1:
                        thd[1] = load_h(1, src)
                    lastmm = None
                    for b in range(NB):
                        cs = slice(b * 512, (b + 1) * 512)
                        hcur = hnb[b % 2]
                        if b + 1 < NB:
                            sqt_n = norm_squares(hbs[(b + 1) % 2], 512, sq8, [thd[b + 1]] + A["sq_free"], True)
                        for kp in range(NKC // 2):
                            wi, wt, wtok = feeder.next()
                            for j_ in range(2):
                                tm, t3 = proj_rope(wt, wtok, tn, tt, KT[:, 2 * kp + j_, cs], A["kt_free"], hn=hcur, wcol=j_)
                                kt_tok = t3
                                lastmm = tm
                            feeder.release(wi, [tm])
                        A["tab_free"] = [DVE.last]
                        if b + 1 < NB:
                            tt_n = load_tabs(b + 1)
                            tn_n, tmm = norm_finish(hbs[(b + 1) % 2], 512, 0, spl, hnb[(b + 1) % 2], sq8, sqt_n, [thd[b + 1]], hnfree[(b + 1) % 2])
                            A["sq_free"] = [tmm]
                            A["hb_free"][(b + 1) % 2] = [tn_n]
                            if b + 2 < NB:
                                thd[b + 2] = load_h(b + 2, src)
                        for vt in range(NVT):
                            wi, wt, wtok = feeder.next()
                            w3 = wt[:, 0:2048].rearrange("p (c n) -> p c n", c=8)
                            ps, pfree = pp.alloc()
                            PE.waits(pfree + [wtok, tn])
                            ins = None
                            for tq in range(4):
                                for c in range(8):
                                    ins = PE.e.matmul(ps[:, tq * 256:(tq + 1) * 256], hcur[:, c, tq * 128:(tq + 1) * 128], w3[:, c, :], start=(c == 0), stop=(c == 7))
                            tm = PE.done(ins)
                            feeder.release(wi, [tm])
                            te = op(ACT, [tm] + A["kt_free"], lambda e: e.activation(out=V[:, b * 4:(b + 1) * 4, vt * 256:(vt + 1) * 256], in_=ps[:, 0:1024].rearrange("p (a n) -> p a n", a=4), func=AF.Copy))
                            pp.release(ps, [te])
                            v_tok = te
                            lastmm = tm
                        hnfree[b % 2] = [lastmm]
                        if b + 1 < NB:
                            tn, tt = tn_n, tt_n
                    A["hn_free"] = [lastmm]
                    A["ot_free"] = [[lastmm], [lastmm]]
                    kv_ready = [kt_tok, v_tok, POOL.last, ACT.last]

                    use_acc = (lt == 0 and half == 1)
                    groups = [(q_,) for q_ in range(NQC)] if lt == 0 else [(2 * q_, 2 * q_ + 1) for q_ in range(NQC // 2)]
                    NG = len(groups)

                    def pro_norm(b, th, sqt):
                        tn, tmm = norm_finish(hbs[b % 2], 512, 0, spl, hn, sq8, sqt, [th], A["hn_free"])
                        A["sq_free"] = [tmm]
                        return tn

                    def pro_q(b, tn, th, tt):
                        cs = slice(b * 512, (b + 1) * 512)
                        qtoks = []
                        lastq = None
                        for qp in range(NQC // 2):
                            wi, wt, wtok = feeder.next()
                            for j_ in range(2):
                                tm, t3 = proj_rope(wt, wtok, tn, tt, QT[:, 2 * qp + j_, :], A["qt_free"], wcol=j_)
                                qtoks.append(t3)
                                lastq = tm
                            feeder.release(wi, [tm])
                        A["tab_free"] = [DVE.last]
                        A["hn_free"] = [lastq]
                        if use_acc:
                            th = dma(SP, [tn], d_h[b % 2], hbs[b % 2][:], hview(hY)[:, :, cs])
                        return th, qtoks

                    def attention_group(b, qcs, qtoks):
                        OT = OTs[b % 2]
                        items = []
                        for qc in qcs:
                            if lt == 0:
                                ks = [(kc, 0, 512, 0) for kc in range(NT)]
                            else:
                                base = b * 4
                                rng = {-1: (0, 128), 0: (0, 256), 1: (0, 512), 3: (256, 512), 4: (384, 512), 2: (0, 512)}
                                ks = [(base + r, rng[r][0], rng[r][1], r + 1) for r in (1, -1, 0, 3, 4, 2) if 0 <= base + r < NT]
                            for i_, k_ in enumerate(ks):
                                items.append((qc, k_, i_ == 0, i_ == len(ks) - 1))
                        accd = {}
                        last_sc = [None]

                        def emit_av(pd):
                            qc, (kc, qa, qb, r), first, last, P, tokp, tacc = pd
                            ac = accd[qc]
                            O_db = ac["O"]
                            PE.waits(tokp + (ac["free"] if first else []) + kv_ready)
                            PE.wait(const_ready)
                            if lt == 0:
                                R_db = ac["R"]
                                vl = V[:, kc, qc * 128:(qc + 1) * 128]
                                PE.e.matmul(O_db[:, 0:512], vl, P[:, 0:512], start=first, stop=last)
                                PE.e.matmul(O_db[:, 512:1024], vl, P[:, 512:1024], start=first, stop=last)
                                ins = PE.e.matmul(R_db[:, 512:1024], ones[:], P[:, 512:1024], start=first, stop=last)
                            else:
                                gA = 0 if qc < 4 else 2
                                gB = gA + 1
                                PE.e.matmul(O_db[0:64, qa:qb], V[:, kc, gA * 64:(gA + 1) * 64], P[:, qa:qb], start=first, stop=last)
                                PE.e.matmul(O_db[64:128, qa:qb], V[:, kc, gB * 64:(gB + 1) * 64], P[:, 512 + qa:512 + qb], start=first, stop=last)
                                PE.e.matmul(O_db[0:64, 512 + qa:512 + qb], ones[:, 0:64], P[:, qa:qb], start=first, stop=last)
                                ins = PE.e.matmul(O_db[64:128, 512 + qa:512 + qb], ones[:, 0:64], P[:, 512 + qa:512 + qb], start=first, stop=last)
                            tk = PE.done(ins)
                            pr.release(P, [tk] + ([tacc] if tacc is not None else []))
                            if last:
                                epilogue(qc, tk, tacc)

                        def epilogue(qc, av_tok, tacc):
                            ac = accd[qc]
                            O_db = ac["O"]
                            if lt == 0:
                                R_db = ac["R"]
                                od = ods[b % 2]
                                acc1 = ac["acc1"]
                                PE.waits([tacc, av_tok, const_ready])
                                tokR = PE.done(PE.e.matmul(R_db[:, 0:512], onesF[:], acc1[:], start=True, stop=True))
                                acc_state["free"][ac["ai"]] = [tokR]
                                tc_ = op(DVE, [av_tok], lambda e: e.tensor_copy(out=o12[:], in_=O_db[:, :]))
                                pp.release(O_db, [tc_])
                                t1 = op(ACT, [tokR, DVE.last], lambda e: e.activation(out=rec[:], in_=R_db[:, :], func=AF.Ln))
                                pp.release(R_db, [t1])
                                t1 = op(ACT, [t1], lambda e: e.activation(out=rec[:], in_=rec[:], func=AF.Exp, scale=-1.0))
                                t2 = op(DVE, [t1, tc_], lambda e: e.tensor_tensor(out=o12[:], in0=o12[:], in1=rec[:], op=ALU.mult))
                                op(DVE, [t2, lay_ready] + A["od_free"][b % 2], lambda e: e.scalar_tensor_tensor(out=od[:, qc, :], in0=o12[:, 512:1024], scalar=nlam, in1=o12[:, 0:512], op0=ALU.mult, op1=ALU.add))
                            else:
                                t1 = op(DVE, [av_tok, lay_ready], lambda e: e.tensor_scalar(out=rec[:, 0:512], in0=O_db[:, 512:1024], scalar1=es_t[:, qc:qc + 1], scalar2=None, op0=ALU.add))
                                t1 = op(ACT, [t1], lambda e: e.activation(out=rec[:, 0:512], in_=rec[:, 0:512], func=AF.Ln))
                                t1 = op(ACT, [t1], lambda e: e.activation(out=rec[:, 0:512], in_=rec[:, 0:512], func=AF.Exp, scale=-1.0))
                                t2 = op(DVE, [t1] + A["ot_free"][b % 2], lambda e: e.tensor_tensor(out=OT[:, qc, :], in0=O_db[:, 0:512], in1=rec[:, 0:512], op=ALU.mult))
                                pp.release(O_db, [t2])

                        pendq = []
                        DEPTH = 2
                        tacc = None
                        for (qc, k_, first, last) in items:
                            kc, qa, qb, r = k_
                            if first:
                                ac = {}
                                ac["O"], ofree = pp.alloc()
                                ac["free"] = list(ofree)
                                if lt == 0:
                                    ac["R"], rfree = pp.alloc()
                                    ac["free"] += rfree
                                    ac["ai"] = acc_state["n"] % 2
                                    acc_state["n"] += 1
                                    ac["acc1"] = accs[ac["ai"]]
                                accd[qc] = ac
                            kch = qc if lt == 0 else (0 if qc < 4 else 1)
                            S_db, sfree = pp.alloc()
                            PE.waits(sfree + [qtoks[qc]] + kv_ready)
                            ks = slice(kc * 128, (kc + 1) * 128)
                            if lt == 0:
                                PE.e.matmul(S_db[:, qa:qb], KT[0:64, kch, ks], QT[0:64, qc, qa:qb], start=True, stop=True)
                                ts_ = PE.done(PE.e.matmul(S_db[:, 512 + qa:512 + qb], KT[64:128, kch, ks], QT[64:128, qc, qa:qb], start=True, stop=True))
                            else:
                                PE.wait(tmk)
                                PE.e.matmul(S_db[:, qa:qb], KT[0:64, kch, ks], QT[0:64, qc, qa:qb], start=True, stop=False)
                                PE.e.matmul(S_db[:, 512 + qa:512 + qb], KT[64:128, kch, ks], QT[64:128, qc, qa:qb], start=True, stop=False)
                                PE.e.matmul(S_db[:, qa:qb], ident, mask[:, r, qa:qb], start=False, stop=True)
                                ts_ = PE.done(PE.e.matmul(S_db[:, 512 + qa:512 + qb], ident, mask[:, r, qa:qb], start=False, stop=True))
                            last_sc[0] = ts_
                            if len(pendq) >= DEPTH:
                                emit_av(pendq.pop(0))
                            P, pfree = pr.alloc()
                            if qa == 0 and qb == 512:
                                te = op(ACT, [ts_] + pfree, lambda e: e.activation(out=P[:, :], in_=S_db[:, :], func=AF.Exp, scale=0.125))
                            else:
                                te = op(ACT, [ts_] + pfree, lambda e: e.activation(out=P[:, :].rearrange("p (a n) -> p a n", a=2)[:, :, qa:qb], in_=S_db[:, :].rearrange("p (a n) -> p a n", a=2)[:, :, qa:qb], func=AF.Exp, scale=0.125))
                            pp.release(S_db, [te])
                            if lt == 1:
                                pendq.append((qc, k_, first, last, P, [te], None))
                            else:
                                acc1 = accd[qc]["acc1"]
                                if first:
                                    tacc = op(DVE, [te] + acc_state["free"][accd[qc]["ai"]], lambda e: e.tensor_copy(out=acc1[:], in_=P[:, 0:512]))
                                else:
                                    tacc = op(DVE, [te, tacc], lambda e: e.tensor_tensor(out=acc1[:], in0=acc1[:], in1=P[:, 0:512], op=ALU.add))
                                pendq.append((qc, k_, first, last, P, [te], tacc))
                        while pendq:
                            emit_av(pendq.pop(0))
                        return last_sc[0]

                    def subln_squares(b):
                        od = ods[b % 2]
                        toks = []
                        for qc in range(NQC):
                            toks.append(op(POOL, [DVE.last] + A["sq_free"], lambda e: e.tensor_tensor(out=sq8[:, qc, :], in0=od[:, qc, :], in1=od[:, qc, :], op=ALU.mult)))
                        return toks

                    def epi_subln(b, sqtoks):
                        od = ods[b % 2]
                        OT = OTs[b % 2]
                        tm = None
                        for pair in range(NQC // 2):
                            ps, pfree = pp.alloc()
                            PE.waits(pfree + sqtoks + [const_ready])
                            PE.e.matmul(ps[:, 0:512], onesH[:], sq8[:, 2 * pair, :], start=True, stop=True)
                            tm = PE.done(PE.e.matmul(ps[:, 512:1024], onesH[:], sq8[:, 2 * pair + 1, :], start=True, stop=True))
                            t1 = op(ACT, [tm, DVE.last], lambda e: e.activation(out=rec[:], in_=ps[:, :], func=AF.Ln, bias=eps_ap, scale=1.0))
                            pp.release(ps, [t1])
                            t2 = op(ACT, [t1], lambda e: e.activation(out=rec[:], in_=rec[:], func=AF.Exp, scale=-0.5))
                            for j_ in range(2):
                                qc = 2 * pair + j_
                                op(DVE, [t2, lay_ready] + A["ot_free"][b % 2], lambda e: e.scalar_tensor_tensor(out=OT[:, qc, :], in0=od[:, qc, :], scalar=sublnS, in1=rec[:, j_ * 512:(j_ + 1) * 512], op0=ALU.mult, op1=ALU.mult))
                        A["sq_free"] = [tm]
                        A["od_free"][b % 2] = [DVE.last, POOL.last]

                    def epi_wo(b, th_acc, part, nparts):
                        OT = OTs[b % 2]
                        hb = hbs[b % 2]
                        cs = slice(b * 512, (b + 1) * 512)
                        ot_ready = DVE.last
                        ntile = 2 if lt == 0 else 4
                        per = ntile // nparts
                        last_mm = None
                        for ti in range(part * per, (part + 1) * per):
                            wi, wt, wtok = feeder.next()
                            if lt == 0:
                                w3 = wt[:, 0:2048].rearrange("p (c n) -> p c n", c=4)
                                for mh in range(2):
                                    ps, pfree = pp.alloc()
                                    PE.waits(pfree + [wtok, ot_ready])
                                    for mm in range(2):
                                        mloc = 2 * mh + mm
                                        for c in range(4):
                                            ins = PE.e.matmul(ps[:, mm * 512:(mm + 1) * 512], w3[:, c, mloc * 128:(mloc + 1) * 128], OT[:, c, :], start=(c == 0), stop=(c == 3))
                                    tm = PE.done(ins)
                                    for mm in range(2):
                                        m = 4 * ti + 2 * mh + mm
                                        ta = op(DVE, [tm, th_acc], lambda e: e.tensor_tensor(out=hb[:, m, :], in0=ps[:, mm * 512:(mm + 1) * 512], in1=hb[:, m, :], op=ALU.add))
                                    pp.release(ps, [ta])
                                feeder.release(wi, [tm])
                            else:
                                mp = ti
                                w3 = wt[:, 0:2048].rearrange("p (c n) -> p c n", c=8)
                                ps, pfree = pp.alloc()
                                PE.waits(pfree + [wtok, ot_ready])
                                for mm in range(2):
                                    for c in range(8):
                                        ins = PE.e.matmul(ps[:, mm * 512:(mm + 1) * 512], w3[:, c, mm * 128:(mm + 1) * 128], OT[:, c, :], start=(c == 0), stop=(c == 7))
                                tm = PE.done(ins)
                                feeder.release(wi, [tm])
                                for mm in range(2):
                                    m = 2 * mp + mm
                                    ta = op(DVE, [tm, th_acc], lambda e: e.tensor_tensor(out=hb[:, m, :], in0=ps[:, mm * 512:(mm + 1) * 512], in1=hb[:, m, :], op=ALU.add))
                                pp.release(ps, [ta])
                            last_mm = tm
                        if part == nparts - 1:
                            A["ot_free"][b % 2] = [last_mm]
                            tst = dma(SP, [DVE.last], d_st[b % 2], hview(hY)[:, :, cs], hb[:])
                            A["hb_free"][b % 2] = [tst]

                    def epi_list(b, th_acc):
                        parts = []
                        if lt == 0:
                            sqtoks = subln_squares(b)
                            parts.append(lambda: epi_subln(b, sqtoks))
                            parts.append(lambda: epi_wo(b, th_acc, 0, 1))
                        else:
                            parts.append(lambda: epi_wo(b, th_acc, 0, 2))
                            parts.append(lambda: epi_wo(b, th_acc, 1, 2))
                        return parts

                    A["hb_free"] = [[DVE.last, POOL.last], [DVE.last, POOL.last]]
                    if half == halves[0] and li + 1 < L:
                        queue_casts(li + 1)
                    th = load_h(0, src)
                    tt = load_tabs(0)
                    sqt = norm_squares(hbs[0], 512, sq8, [th] + A["sq_free"], True)
                    tn = pro_norm(0, th, sqt)
                    cur = pro_q(0, tn, th, tt)
                    pending = []
                    for b in range(NB):
                        th_acc, qtoks = cur
                        nxt = {}
                        sched = [[] for _ in range(NG)]
                        epi = list(pending)
                        if lt == 0:
                            for i_, f_ in enumerate(epi):
                                sched[i_] += [f_]
                            k = 1 if epi else 0
                        else:
                            for i_, f_ in enumerate(epi):
                                sched[i_] += [f_]
                            k = max(0, len(epi) - 1)
                        if b + 1 < NB:
                            def _loads(bb=b + 1):
                                nxt["th"] = load_h(bb, src)
                                nxt["tt"] = load_tabs(bb)
                            def _squares(bb=b + 1):
                                nxt["sqt"] = norm_squares(hbs[bb % 2], 512, sq8, [nxt["th"]] + A["sq_free"], False)
                            def _norm(bb=b + 1):
                                nxt["tn"] = pro_norm(bb, nxt["th"], nxt["sqt"])
                            sched[k] += [_loads, _squares]
                            sched[k + 1] += [_norm]
                        att_last = None
                        for gi_, grp in enumerate(groups):
                            att_last = attention_group(b, grp, qtoks)
                            pop_cast()
                            for f_ in sched[gi_]:
                                f_()
                        A["qt_free"] = [att_last]
                        pending = epi_list(b, th_acc)
                        if b + 1 < NB:
                            cur = pro_q(b + 1, nxt["tn"], nxt["th"], nxt["tt"])
                    for f in pending:
                        f()
                    while cast_queue:
                        pop_cast()
                    barrier()
                    A = {"hb_free": [[], []], "hn_free": [], "qt_free": [], "ot_free": [[], []], "od_free": [[], []],
                         "tab_free": [], "sq_free": [], "kt_free": []}
                barrier()

            with ExitStack() as fsc:
                NW = n_win
                hwS = [[sb(f"hw{i}", [128, 8, 512], F32, fsc) for i in range(NW)] for _ in range(2)]
                pwS = [[sb(f"pw{i}", [128, 2, 512], BF16, fsc) for i in range(NW)] for _ in range(2)]
                hnw = [sb(f"hnw{i}", [128, 8, 512], BF16, fsc) for i in range(NW)]
                aT = [sb(f"aT{i}", [128, NFC, 512], BF16, fsc) for i in range(NW)]
                sq8f = sb("sq8f", [128, 8, 512], BF16, fsc)
                cv_items = [(sb(f"tg{i}", [128, 512], F32, fsc), sb(f"tv{i}", [128, 512], F32, fsc), sb(f"sg{i}", [128, 512], F32, fsc)) for i in range(3)]
                cvr = Fifo(cv_items)
                pl_items = [(sb(f"sgm{i}", [128, 512], F32, fsc), sb(f"ptmp{i}", [128, 512], F32, fsc)) for i in range(2)]
                plr = Fifo(pl_items)
                dst = dr["outT"] if is_last else hX
                wl = window_list(S)
                passes = [wl[p0:p0 + NW] for p0 in range(0, len(wl), NW)]
                hw_free = [[[] for _ in range(NW)] for _ in range(2)]
                pw_free = [[[] for _ in range(NW)] for _ in range(2)]
                hn_free = [[] for _ in range(NW)]
                a_free = [[] for _ in range(NW)]
                B = {"sq_free": []}
                cw = lambda t, fc: spl[:, 32 + 44 * t + fc: 32 + 44 * t + fc + 1]
                cbv = lambda fc: spl[:, 164 + fc: 165 + fc]

                def issue_loads(pi):
                    st_ = pi % 2
                    h_ready = []
                    p_ready = []
                    for wi_, (s_, e_) in enumerate(passes[pi]):
                        W = e_ - s_
                        hwt = hwS[st_][wi_]
                        lo = s_ - 1
                        hi = e_ + 1
                        deps = list(hw_free[st_][wi_])
                        toks = []
                        c0 = 0
                        if lo < 0:
                            toks.append(op(DVE, deps, lambda e: e.memset(hwt[:, :, 0:1], 0.0)))
                            lo = 0
                            c0 = 1
                        if hi > S:
                            toks.append(op(DVE, deps, lambda e: e.memset(hwt[:, :, W + 1:W + 2], 0.0)))
                            hi = S
                        toks.append(dma(SP, deps, d_h[wi_], hwt[:, :, c0:c0 + (hi - lo)], hview(hY)[:, :, lo:hi]))
                        p_ready.append(dma(POOL, pw_free[st_][wi_], d_p[wi_], pwS[st_][wi_][:, :, 0:W], dr["pT"][li].rearrange("(c p) t -> p c t", p=128)[:, :, s_:e_]))
                        h_ready.append(toks)
                    return h_ready, p_ready

                loads = issue_loads(0)
                for pi, wins in enumerate(passes):
                    st_ = pi % 2
                    hw = hwS[st_]
                    pw = pwS[st_]
                    nw = len(wins)
                    Wd = [e_ - s_ for (s_, e_) in wins]
                    h_ready, p_ready = loads
                    hn_ready = []
                    for wi_ in range(nw):
                        sqt = norm_squares(hw[wi_], Wd[wi_] + 2, sq8f, h_ready[wi_] + B["sq_free"], True)
                        tn, tmm = norm_finish(hw[wi_], Wd[wi_] + 2, 8, spl, hnw[wi_], sq8f, sqt, h_ready[wi_], hn_free[wi_])
                        B["sq_free"] = [tmm]
                        hn_ready.append(tn)
                    a_tok = [None] * nw
                    for jf in range(NFC):
                        fi, wt, wtok = feeder.next()
                        w3 = wt[:, 0:2048].rearrange("p (c n) -> p c n", c=8)
                        tm = None
                        for wi_ in range(nw):
                            W = Wd[wi_]
                            ps, pfree = pp.alloc()
                            PE.waits(pfree + [wtok, hn_ready[wi_]])
                            for c in range(8):
                                PE.e.matmul(ps[:, 0:W + 2], w3[:, c, 0:128], hnw[wi_][:, c, 0:W + 2], start=(c == 0), stop=(c == 7))
                            for c in range(8):
                                ins = PE.e.matmul(ps[:, 512:512 + W + 2], w3[:, c, 128:256], hnw[wi_][:, c, 0:W + 2], start=(c == 0), stop=(c == 7))
                            tm = PE.done(ins)
                            (tg, tv, sg), cfree = cvr.alloc()
                            fg, fv = jf, NFC + jf
                            g1 = op(ACT, [tm] + cfree, lambda e: e.activation(out=tg[:, 0:W], in_=ps[:, 0:W], func=AF.Identity, scale=cw(0, fg), bias=cbv(fg)))
                            v1 = op(ACT, [tm] + cfree, lambda e: e.activation(out=tv[:, 0:W], in_=ps[:, 512:512 + W], func=AF.Identity, scale=cw(0, fv), bias=cbv(fv)))
                            g2 = op(DVE, [g1], lambda e: e.scalar_tensor_tensor(out=tg[:, 0:W], in0=ps[:, 1:W + 1], scalar=cw(1, fg), in1=tg[:, 0:W], op0=ALU.mult, op1=ALU.add))
                            v2 = op(DVE, [v1], lambda e: e.scalar_tensor_tensor(out=tv[:, 0:W], in0=ps[:, 513:513 + W], scalar=cw(1, fv), in1=tv[:, 0:W], op0=ALU.mult, op1=ALU.add))
                            g3 = op(DVE, [g2], lambda e: e.scalar_tensor_tensor(out=tg[:, 0:W], in0=ps[:, 2:W + 2], scalar=cw(2, fg), in1=tg[:, 0:W], op0=ALU.mult, op1=ALU.add))
                            v3 = op(DVE, [v2], lambda e: e.scalar_tensor_tensor(out=tv[:, 0:W], in0=ps[:, 514:514 + W], scalar=cw(2, fv), in1=tv[:, 0:W], op0=ALU.mult, op1=ALU.add))
                            pp.release(ps, [g3, v3])
                            s1 = op(ACT, [g3], lambda e: e.activation(out=sg[:, 0:W], in_=tg[:, 0:W], func=AF.Silu))
                            a_tok[wi_] = op(POOL, [s1, v3] + a_free[wi_], lambda e: e.tensor_tensor(out=aT[wi_][:, jf, 0:W], in0=sg[:, 0:W], in1=tv[:, 0:W], op=ALU.mult))
                            cvr.release((tg, tv, sg), [a_tok[wi_]])
                        feeder.release(fi, [tm])
                    loads_next = issue_loads(pi + 1) if pi + 1 < len(passes) else None
                    h2_tok = [None] * nw
                    for m in range(8):
                        dbs = [None] * nw
                        for half in range(2):
                            fi, wt, wtok = feeder.next()
                            w3 = wt[:, 0:1408].rearrange("p (c n) -> p c n", c=11)
                            tm = None
                            for wi_ in range(nw):
                                W = Wd[wi_]
                                if half == 0:
                                    ps, pfree = pp.alloc()
                                    dbs[wi_] = ps
                                    PE.waits(pfree)
                                ps = dbs[wi_]
                                PE.waits([wtok, a_tok[wi_]])
                                for kk in range(11):
                                    ins = PE.e.matmul(ps[:, 0:W], w3[:, kk, :], aT[wi_][:, half * 11 + kk, 0:W], start=(half == 0 and kk == 0), stop=(half == 1 and kk == 10))
                                tm = PE.done(ins)
                                if half == 1:
                                    h2_tok[wi_] = op(DVE, [tm] + h_ready[wi_] + [hn_ready[wi_]], lambda e: e.tensor_tensor(out=hw[wi_][:, m, 1:W + 1], in0=ps[:, 0:W], in1=hw[wi_][:, m, 1:W + 1], op=ALU.add))
                                    pp.release(ps, [h2_tok[wi_]])
                                    a_free[wi_] = [tm]
                            feeder.release(fi, [tm])
                    hn2_ready = []
                    for wi_ in range(nw):
                        sqt = norm_squares(hw[wi_], Wd[wi_], sq8f, [h2_tok[wi_]] + B["sq_free"], True, off=1)
                        tn, tmm = norm_finish(hw[wi_], Wd[wi_], 16, spl, hnw[wi_], sq8f, sqt, [h2_tok[wi_]], [PE.last], off=1)
                        B["sq_free"] = [tmm]
                        hn2_ready.append(tn)
                    gate_tiles = [feeder.next() for _ in range(4)]
                    pj_i, pj_t, pj_tok = feeder.next()
                    pj3 = pj_t[:, 0:2048].rearrange("p (c n) -> p c n", c=2)
                    h3_tok = [None] * nw
                    tm = None
                    for mp in range(4):
                        fi, wt, wtok = gate_tiles[mp]
                        w3 = wt[:, 0:2048].rearrange("p (c n) -> p c n", c=8)
                        for mm in range(2):
                            m = 2 * mp + mm
                            for wi_ in range(nw):
                                W = Wd[wi_]
                                ps, pfree = pp.alloc()
                                PE.waits(pfree + [wtok, pj_tok, hn2_ready[wi_], p_ready[wi_]])
                                for c in range(8):
                                    PE.e.matmul(ps[:, 0:W], w3[:, c, mm * 128:(mm + 1) * 128], hnw[wi_][:, c, 0:W], start=(c == 0), stop=(c == 7))
                                for c in range(2):
                                    ins = PE.e.matmul(ps[:, 512:512 + W], pj3[:, c, m * 128:(m + 1) * 128], pw[wi_][:, c, 0:W], start=(c == 0), stop=(c == 1))
                                tm = PE.done(ins)
                                (sgm, ptmp), lfree = plr.alloc()
                                s1 = op(ACT, [tm] + lfree, lambda e: e.activation(out=sgm[:, 0:W], in_=ps[:, 0:W], func=AF.Sigmoid))
                                s2 = op(DVE, [s1], lambda e: e.tensor_tensor(out=ptmp[:, 0:W], in0=ps[:, 512:512 + W], in1=sgm[:, 0:W], op=ALU.mult))
                                pp.release(ps, [s2])
                                h3_tok[wi_] = op(POOL, [s2, hn2_ready[wi_]], lambda e: e.tensor_tensor(out=hw[wi_][:, m, 1:W + 1], in0=hw[wi_][:, m, 1:W + 1], in1=ptmp[:, 0:W], op=ALU.add))
                                plr.release((sgm, ptmp), [h3_tok[wi_]])
                        feeder.release(fi, [tm])
                    feeder.release(pj_i, [tm])
                    for wi_, (s_, e_) in enumerate(wins):
                        W = Wd[wi_]
                        tk = h3_tok[wi_]
                        if is_last and apply_final:
                            tk = norm_block(hw[wi_], W, 24, spl, None, [tk], [], off=1, out_f32=hw[wi_])
                        tst = dma(SP, [tk, POOL.last], d_st[wi_], hview(dst)[:, :, s_:e_], hw[wi_][:, :, 1:W + 1])
                        hw_free[st_][wi_] = [tst]
                        pw_free[st_][wi_] = [PE.last]
                        hn_free[wi_] = [PE.last]
                    loads = loads_next
                barrier()
        SP.wait(d_st[0].tok())
        SP.wait(d_st[1].tok())
    return nc


LAYER_TYPES = [0, 1, 0, 1]
_PROGRAM_CACHE = {}


def _get_program(S, layers, apply_final, n_win):
    key = (S, tuple(layers), apply_final, n_win)
    if key not in _PROGRAM_CACHE:
        _PROGRAM_CACHE[key] = build_program(S, list(layers), apply_final, n_win)
    return _PROGRAM_CACHE[key]


def run_layers(xT_list, inp, layer_ids, pos_list, apply_final, n_win=2, core_ids=None):
    S = xT_list[0].shape[1]
    layers = [(gi, LAYER_TYPES[gi]) for gi in layer_ids]
    nc = _get_program(S, layers, apply_final, n_win)
    cst, msk = host_consts()
    packs = [pack_layer(LAYER_TYPES[gi], gi // 2, inp, gi) for gi in layer_ids]
    in_maps = []
    ncore = len(xT_list)
    for ci in range(ncore):
        m = {"xT": np.ascontiguousarray(xT_list[ci], dtype=np.float32), "pos": pos_list[ci], "cst": cst, "msk": msk, "perm": host_perm()}
        m["pT"] = np.ascontiguousarray(np.stack([inp["p"][gi][ci].T for gi in layer_ids]), dtype=np.float32)
        for li, pk in enumerate(packs):
            m[f"wkv{li}"] = pk["wkv"]
            m[f"wa{li}"] = pk["wa"]
            m[f"wb{li}"] = pk["wb"]
            m[f"sp{li}"] = pk["sp"]
        in_maps.append(m)
    res = run_bass_kernel_spmd(nc, in_maps, core_ids=list(range(ncore)) if core_ids is None else core_ids)
    return [r["outT"] for r in res.results]


FUSED = True


def kernel(**inputs):
    inp = {k: np.asarray(v) for k, v in inputs.items()}
    x = inp["x"]
    B, S, _ = x.shape
    xT = [np.ascontiguousarray(x[b].T) for b in range(B)]
    pos = [np.ascontiguousarray(inp["positions"][b].reshape(1, S).astype(np.int32)) for b in range(B)]
    if FUSED:
        outs = run_layers(xT, inp, [0, 1, 2, 3], pos, True)
    else:
        cur = xT
        for gi in range(4):
            cur = run_layers(cur, inp, [gi], pos, gi == 3)
        outs = cur
    return np.stack([o.T for o in outs]).astype(np.float32)
```
